# Optimizing a Trainium2 kernel written in Bass

```python
import jax, jax.numpy as jnp
from jax import lax
import numpy as np

D_MODEL = 1024
BATCH = 2
SEQ = 8192
DEPTH = 4
DEC_BATCH = 32
DEC_SEQ = 16
PAST_LEN = 1024

CHUNK = 64
N_MIXERS = 2
N_A_LAYERS = (DEPTH + 1) // 2
N_B_LAYERS = DEPTH // 2
CONV_WIDTH = 4
EPS = 1e-6
D_RNN = D_MODEL
LRU_BLOCKS = 4
LRU_BLOCK_W = D_RNN // LRU_BLOCKS
LRU_C = 8.0
SSD_EXPAND = 2
D_INNER = SSD_EXPAND * D_MODEL
SSD_HEAD_DIM = 64
SSD_HEADS = D_INNER // SSD_HEAD_DIM
SSD_GROUPS = 4
SSD_HPG = SSD_HEADS // SSD_GROUPS
SSD_STATE = 128
SSD_GN = SSD_GROUPS * SSD_STATE
SSD_CONV_DIM = D_INNER + 2 * SSD_GN
SSD_IN_DIM = D_INNER + SSD_CONV_DIM + SSD_HEADS
D_FF = -(-8 * D_MODEL // (3 * 256)) * 256

kernel_name = 'hybrid_rglru_ssd_stream_step'


def rmsnorm(x, g):
    xf = x.astype(jnp.float32)
    y = xf * lax.rsqrt(jnp.mean(xf * xf, axis=-1, keepdims=True) + EPS)
    return (y * g.astype(jnp.float32)).astype(x.dtype)


def causal_conv(inp, prev, w, b):
    L = inp.shape[1]
    full = jnp.concatenate([prev.astype(inp.dtype), inp], axis=1)
    out = b + full[:, 0:L] * w[0]
    for k in range(1, CONV_WIDTH):
        out = out + full[:, k:k + L] * w[k]
    return out, full[:, L:]


def _lin_combine(e1, e2):
    a1, b1 = e1
    a2, b2 = e2
    return a1 * a2, a2 * b1 + b2


def rglru_mixer(h, conv_prev, h_prev, w_in, conv_w, conv_b, w_r, b_r, w_i, b_i, lam, w_out):
    bsz, L, _ = h.shape
    proj = h @ w_in
    gate_br, rec_br = proj[..., :D_RNN], proj[..., D_RNN:]
    xc, new_conv = causal_conv(rec_br, conv_prev, conv_w, conv_b)
    xb = xc.reshape(bsz, L, LRU_BLOCKS, LRU_BLOCK_W)
    r = jax.nn.sigmoid(jnp.einsum('blki,kij->blkj', xb, w_r) + b_r).reshape(bsz, L, D_RNN)
    i = jax.nn.sigmoid(jnp.einsum('blki,kij->blkj', xb, w_i) + b_i).reshape(bsz, L, D_RNN)
    log_a = -LRU_C * r.astype(jnp.float32) * jax.nn.softplus(-lam.astype(jnp.float32))
    a = jnp.exp(log_a)
    u = jnp.sqrt(-jnp.expm1(2.0 * log_a)) * (i * xc).astype(jnp.float32)
    a_cum, u_cum = lax.associative_scan(_lin_combine, (a, u), axis=1)
    hs = u_cum + a_cum * h_prev.astype(jnp.float32)[:, None]
    out = (hs.astype(h.dtype) * jax.nn.gelu(gate_br, approximate=True)) @ w_out
    return out, new_conv, hs[:, -1].astype(h.dtype)


def ssd_scan(x, dt, A, Bm, Cm, s0):
    b, L = x.shape[0], x.shape[1]
    q = CHUNK if L % CHUNK == 0 else L
    c = L // q
    G, R, P, N = SSD_GROUPS, SSD_HPG, SSD_HEAD_DIM, SSD_STATE
    dtr = dt.reshape(b, c, q, G, R)
    dA = dtr * A.reshape(G, R)
    xdt = x.reshape(b, c, q, G, R, P) * dtr[..., None]
    Br = Bm.reshape(b, c, q, G, N)
    Cr = Cm.reshape(b, c, q, G, N)
    acs = jnp.cumsum(dA, axis=2)
    seg = acs[:, :, :, None] - acs[:, :, None, :]
    mask = jnp.tril(jnp.ones((q, q), dtype=bool))[:, :, None, None]
    lmat = jnp.exp(jnp.where(mask, seg, -jnp.inf))
    cb = jnp.einsum('bcign,bcjgn->bcijg', Cr, Br)
    y_diag = jnp.einsum('bcijgr,bcjgrp->bcigrp', cb[..., None] * lmat, xdt)
    decay = jnp.exp(acs[:, :, -1:] - acs)
    states = jnp.einsum('bcjgn,bcjgrp->bcgrpn', Br, xdt * decay[..., None])
    blk_decay = jnp.exp(acs[:, :, -1])

    def step(s, inp):
        st, dec = inp
        return s * dec[..., None, None] + st, s

    s_final, s_in = lax.scan(step, s0.reshape(b, G, R, P, N),
                             (jnp.swapaxes(states, 0, 1), jnp.swapaxes(blk_decay, 0, 1)))
    s_in = jnp.swapaxes(s_in, 0, 1)
    y_off = jnp.einsum('bcign,bcgrpn->bcigrp', Cr, s_in) * jnp.exp(acs)[..., None]
    y = (y_diag + y_off).reshape(b, L, SSD_HEADS, P)
    return y, s_final.reshape(b, SSD_HEADS, P, N)


def ssd_mixer(h, conv_prev, s_prev, w_in, conv_w, conv_b, dt_bias, a_log, d_skip, norm_g, w_out):
    bsz, L, _ = h.shape
    proj = h @ w_in
    z = proj[..., :D_INNER]
    xbc = proj[..., D_INNER:D_INNER + SSD_CONV_DIM]
    dt_raw = proj[..., D_INNER + SSD_CONV_DIM:]
    xbc_c, new_conv = causal_conv(xbc, conv_prev, conv_w, conv_b)
    xbc_c = jax.nn.silu(xbc_c).astype(jnp.float32)
    xs = xbc_c[..., :D_INNER].reshape(bsz, L, SSD_HEADS, SSD_HEAD_DIM)
    Bm = xbc_c[..., D_INNER:D_INNER + SSD_GN].reshape(bsz, L, SSD_GROUPS, SSD_STATE)
    Cm = xbc_c[..., D_INNER + SSD_GN:].reshape(bsz, L, SSD_GROUPS, SSD_STATE)
    dt = jax.nn.softplus(dt_raw.astype(jnp.float32) + dt_bias.astype(jnp.float32))
    A = -jnp.exp(a_log.astype(jnp.float32))
    y, s_new = ssd_scan(xs, dt, A, Bm, Cm, s_prev.astype(jnp.float32))
    y = y + d_skip.astype(jnp.float32)[:, None] * xs
    y = y.reshape(bsz, L, D_INNER) * jax.nn.silu(z.astype(jnp.float32))
    yg = y.reshape(bsz, L, SSD_GROUPS, D_INNER // SSD_GROUPS)
    yg = yg * lax.rsqrt(jnp.mean(yg * yg, axis=-1, keepdims=True) + EPS)
    y = (yg.reshape(bsz, L, D_INNER) * norm_g.astype(jnp.float32)).astype(h.dtype)
    return y @ w_out, new_conv, s_new.astype(h.dtype)


def swiglu(h, w_gate, w_up, w_down):
    return (jax.nn.silu(h @ w_gate) * (h @ w_up)) @ w_down


def trunk(x, lru_conv, lru_h, ssd_conv, ssd_s, p):
    new_lc, new_lh, new_sc, new_ss = [], [], [], []
    for layer in range(DEPTH):
        j = layer // N_MIXERS
        hn = rmsnorm(x, p['norm_mix_pre'][layer])
        if layer % N_MIXERS == 0:
            mix, c_new, s_new = rglru_mixer(
                hn, lru_conv[j], lru_h[j], p['lru_w_in'][j], p['lru_conv_w'][j], p['lru_conv_b'][j],
                p['lru_w_r'][j], p['lru_b_r'][j], p['lru_w_i'][j], p['lru_b_i'][j],
                p['lru_lambda'][j], p['lru_w_out'][j])
            new_lc.append(c_new)
            new_lh.append(s_new)
        else:
            mix, c_new, s_new = ssd_mixer(
                hn, ssd_conv[j], ssd_s[j], p['ssd_w_in'][j], p['ssd_conv_w'][j], p['ssd_conv_b'][j],
                p['ssd_dt_bias'][j], p['ssd_a_log'][j], p['ssd_d'][j], p['ssd_norm'][j],
                p['ssd_w_out'][j])
            new_sc.append(c_new)
            new_ss.append(s_new)
        x = x + rmsnorm(mix, p['norm_mix_post'][layer])
        hn = rmsnorm(x, p['norm_ffn_pre'][layer])
        f = swiglu(hn, p['ffn_w_gate'][layer], p['ffn_w_up'][layer], p['ffn_w_down'][layer])
        x = x + rmsnorm(f, p['norm_ffn_post'][layer])
    return x, jnp.stack(new_lc), jnp.stack(new_lh), jnp.stack(new_sc), jnp.stack(new_ss)


def setup_inputs(seed: int = 0) -> dict:
    key = jax.random.key(seed)
    ks = iter(jax.random.split(key, 48))
    f32 = jnp.float32

    def nrm(shape, scale):
        return jax.random.normal(next(ks), shape, f32) * scale

    def gain(shape):
        return 1.0 + 0.05 * jax.random.normal(next(ks), shape, f32)

    NA, NB = N_A_LAYERS, N_B_LAYERS
    a0 = jax.random.uniform(next(ks), (NA, D_RNN), f32, 0.9, 0.999)
    dt0 = jnp.exp(jax.random.uniform(next(ks), (NB, SSD_HEADS), f32, np.log(1e-3), np.log(1e-1)))
    return {
        'x_prompt': nrm((BATCH, SEQ, D_MODEL), 1.0),
        'x_sample': nrm((DEC_BATCH, DEC_SEQ, D_MODEL), 1.0),
        'state_lru_conv': nrm((NA, DEC_BATCH, CONV_WIDTH - 1, D_RNN), 1.0),
        'state_lru_h': nrm((NA, DEC_BATCH, D_RNN), 0.5),
        'state_ssd_conv': nrm((NB, DEC_BATCH, CONV_WIDTH - 1, SSD_CONV_DIM), 1.0),
        'state_ssd': nrm((NB, DEC_BATCH, SSD_HEADS, SSD_HEAD_DIM, SSD_STATE), 0.1),
        'norm_mix_pre': gain((DEPTH, D_MODEL)),
        'norm_mix_post': gain((DEPTH, D_MODEL)),
        'norm_ffn_pre': gain((DEPTH, D_MODEL)),
        'norm_ffn_post': gain((DEPTH, D_MODEL)),
        'lru_w_in': nrm((NA, D_MODEL, 2 * D_RNN), D_MODEL ** -0.5),
        'lru_conv_w': nrm((NA, CONV_WIDTH, D_RNN), CONV_WIDTH ** -0.5),
        'lru_conv_b': nrm((NA, D_RNN), 0.01),
        'lru_w_r': nrm((NA, LRU_BLOCKS, LRU_BLOCK_W, LRU_BLOCK_W), LRU_BLOCK_W ** -0.5),
        'lru_b_r': nrm((NA, LRU_BLOCKS, LRU_BLOCK_W), 0.01),
        'lru_w_i': nrm((NA, LRU_BLOCKS, LRU_BLOCK_W, LRU_BLOCK_W), LRU_BLOCK_W ** -0.5),
        'lru_b_i': nrm((NA, LRU_BLOCKS, LRU_BLOCK_W), 0.01),
        'lru_lambda': jnp.log(a0) - jnp.log1p(-a0),
        'lru_w_out': nrm((NA, D_RNN, D_MODEL), D_RNN ** -0.5),
        'ssd_w_in': nrm((NB, D_MODEL, SSD_IN_DIM), D_MODEL ** -0.5),
        'ssd_conv_w': nrm((NB, CONV_WIDTH, SSD_CONV_DIM), CONV_WIDTH ** -0.5),
        'ssd_conv_b': nrm((NB, SSD_CONV_DIM), 0.01),
        'ssd_dt_bias': dt0 + jnp.log(-jnp.expm1(-dt0)),
        'ssd_a_log': jnp.log(jax.random.uniform(next(ks), (NB, SSD_HEADS), f32, 1.0, 16.0)),
        'ssd_d': 1.0 + 0.1 * jax.random.normal(next(ks), (NB, SSD_HEADS), f32),
        'ssd_norm': gain((NB, D_INNER)),
        'ssd_w_out': nrm((NB, D_INNER, D_MODEL), D_INNER ** -0.5),
        'ffn_w_gate': nrm((DEPTH, D_MODEL, D_FF), D_MODEL ** -0.5),
        'ffn_w_up': nrm((DEPTH, D_MODEL, D_FF), D_MODEL ** -0.5),
        'ffn_w_down': nrm((DEPTH, D_FF, D_MODEL), D_FF ** -0.5),
    }


def reference(x_prompt, x_sample, state_lru_conv, state_lru_h, state_ssd_conv, state_ssd,
              norm_mix_pre, norm_mix_post, norm_ffn_pre, norm_ffn_post,
              lru_w_in, lru_conv_w, lru_conv_b, lru_w_r, lru_b_r, lru_w_i, lru_b_i, lru_lambda, lru_w_out,
              ssd_w_in, ssd_conv_w, ssd_conv_b, ssd_dt_bias, ssd_a_log, ssd_d, ssd_norm, ssd_w_out,
              ffn_w_gate, ffn_w_up, ffn_w_down):
    p = dict(norm_mix_pre=norm_mix_pre, norm_mix_post=norm_mix_post,
             norm_ffn_pre=norm_ffn_pre, norm_ffn_post=norm_ffn_post,
             lru_w_in=lru_w_in, lru_conv_w=lru_conv_w, lru_conv_b=lru_conv_b,
             lru_w_r=lru_w_r, lru_b_r=lru_b_r, lru_w_i=lru_w_i, lru_b_i=lru_b_i,
             lru_lambda=lru_lambda, lru_w_out=lru_w_out,
             ssd_w_in=ssd_w_in, ssd_conv_w=ssd_conv_w, ssd_conv_b=ssd_conv_b,
             ssd_dt_bias=ssd_dt_bias, ssd_a_log=ssd_a_log, ssd_d=ssd_d, ssd_norm=ssd_norm,
             ssd_w_out=ssd_w_out, ffn_w_gate=ffn_w_gate, ffn_w_up=ffn_w_up, ffn_w_down=ffn_w_down)
    bp = x_prompt.shape[0]
    dt_ = x_prompt.dtype
    z_lc = jnp.zeros((N_A_LAYERS, bp, CONV_WIDTH - 1, D_RNN), dt_)
    z_lh = jnp.zeros((N_A_LAYERS, bp, D_RNN), dt_)
    z_sc = jnp.zeros((N_B_LAYERS, bp, CONV_WIDTH - 1, SSD_CONV_DIM), dt_)
    z_ss = jnp.zeros((N_B_LAYERS, bp, SSD_HEADS, SSD_HEAD_DIM, SSD_STATE), dt_)
    y_prompt, p_lc, p_lh, p_sc, p_ss = trunk(x_prompt, z_lc, z_lh, z_sc, z_ss, p)
    y_sample, s_lc, s_lh, s_sc, s_ss = trunk(x_sample, state_lru_conv, state_lru_h,
                                             state_ssd_conv, state_ssd, p)
    return (y_prompt, y_sample, p_lc, p_lh, p_sc, p_ss, s_lc, s_lh, s_sc, s_ss)
```

```python
import contextlib
import numpy as np
import concourse.bass as bass
import concourse.mybir as mybir
from concourse.bass_utils import run_bass_kernel_spmd

F32 = mybir.dt.float32
BF16 = mybir.dt.bfloat16
AF = mybir.ActivationFunctionType
ALU = mybir.AluOpType

ENGS = ['tensor', 'vector', 'scalar', 'gpsimd', 'sync']
NDMASEM = 12

DEPTH = 4
NA, NB = 2, 2
DM = 1024
NPT = 2048
NTOK = 2112
NT = 512
DFF = 2816
EPS = 1e-6
RG = [[0, 1, 2, 3], [4, 5, 6, 7]]
SLOT = 5632


class Op:
    __slots__ = ('eng', 'fn', 'deps', 'idx', 'inc', 'semval', 'dma', 'dsem', 'dval', 'dprev')


class Sched:
    def __init__(self):
        self.ops = []
        self.last_w = {}
        self.readers = {}
        self.dma_count = {e: 0 for e in ENGS}
        self.recording = False

    def add(self, eng, fn, reads=(), writes=(), dma=False):
        if self.recording:
            return -1
        op = Op()
        op.eng = eng
        op.fn = fn
        op.idx = len(self.ops)
        op.dma = dma
        op.inc = False
        op.semval = 0
        deps = set()
        for r in reads:
            w = self.last_w.get(r)
            if w is not None:
                deps.add(w)
        for k in writes:
            w = self.last_w.get(k)
            if w is not None:
                deps.add(w)
            for r in self.readers.get(k, ()):
                deps.add(r)
        best = {}
        keep = set()
        for d in deps:
            p = self.ops[d]
            if p.dma:
                keep.add(d)
            elif best.get(p.eng, -1) < d:
                best[p.eng] = d
        keep.update(best.values())
        op.deps = keep
        for r in reads:
            self.readers.setdefault(r, []).append(op.idx)
        for k in writes:
            self.last_w[k] = op.idx
            self.readers[k] = []
        if dma:
            n = self.dma_count[eng]
            self.dma_count[eng] = n + 1
            op.dsem = n % NDMASEM
            op.dval = 16 * (n // NDMASEM + 1)
            op.dprev = op.dval - 16
        self.ops.append(op)
        return op.idx

    def emit(self, nc, stack):
        ops = self.ops
        for op in ops:
            for d in op.deps:
                p = ops[d]
                if p.dma:
                    continue
                if p.eng == 'tensor' and op.eng == 'tensor' and not op.dma:
                    continue
                p.inc = True
        cnt = {e: 0 for e in ENGS}
        for op in ops:
            if op.inc and not op.dma:
                cnt[op.eng] += 1
                op.semval = cnt[op.eng]
        esem = {e: stack.enter_context(nc.semaphore('es_' + e)) for e in ENGS}
        dsem = {}
        for e in ENGS:
            if self.dma_count[e]:
                dsem[e] = [stack.enter_context(nc.semaphore('ds_%s_%d' % (e, i)))
                           for i in range(min(NDMASEM, self.dma_count[e]))]
        per = {e: [] for e in ENGS}
        for op in ops:
            per[op.eng].append(op)
        block = stack.enter_context(nc.Block())
        self.counts = dict(cnt)

        def run(engname, eng):
            waited = {}

            def wait(sem, key, val):
                if val <= 0 or waited.get(key, 0) >= val:
                    return
                waited[key] = val
                eng.wait_ge(sem, val)

            for op in per[engname]:
                for d in sorted(op.deps):
                    p = ops[d]
                    if p.dma:
                        wait(dsem[p.eng][p.dsem], ('d', p.eng, p.dsem), p.dval)
                    else:
                        if p.eng == 'tensor' and engname == 'tensor' and not op.dma:
                            continue
                        wait(esem[p.eng], ('e', p.eng), p.semval)
                if op.dma:
                    wait(dsem[engname][op.dsem], ('d', engname, op.dsem), op.dprev)
                ins = op.fn(eng)
                if op.dma:
                    ins.then_inc(dsem[engname][op.dsem], 16)
                elif op.inc:
                    ins.then_inc(esem[engname], 1)
            if engname == 'sync':
                for e in ENGS:
                    n = self.dma_count[e]
                    for i in range(min(NDMASEM, n)):
                        tot = 16 * ((n - 1 - i) // NDMASEM + 1)
                        wait(dsem[e][i], ('d', e, i), tot)
                for e in ENGS:
                    if e != 'sync' and cnt[e]:
                        wait(esem[e], ('e', e), cnt[e])

        @block.tensor
        def _(e):
            run('tensor', e)

        @block.vector
        def _(e):
            run('vector', e)

        @block.scalar
        def _(e):
            run('scalar', e)

        @block.gpsimd
        def _(e):
            run('gpsimd', e)

        @block.sync
        def _(e):
            run('sync', e)


def _pv_layout():
    items = [('nmp', 32), ('nmq', 32), ('nfp', 32), ('nfq', 32),
             ('lcw', NA * 4 * 8), ('lcb', NA * 8), ('lbr', NA * 8), ('lbi', NA * 8), ('llam', NA * 8),
             ('scw', NB * 4 * 24), ('scb', NB * 24), ('sgn', NB * 16), ('sdx', NB * 16)]
    off = {}
    o = 0
    for k, n in items:
        off[k] = o
        o += n
    return off, o


PVO, NPV = _pv_layout()
PRO = {'dtb': 0, 'alog': NB * 32, 'even': 2 * NB * 32, 'odd': 2 * NB * 32 + 32}
NPR = 2 * NB * 32 + 64
CO = {'ident': 0, 'U64': 128, 'T64': 192, 'CM64': 256, 'U16': 320, 'T16': 384, 'CM16': 448,
      'MREP': 512, 'SUBM': 512 + 5 * 128}
NCO = 512 + 5 * 128 + 4


def build_program(depth=DEPTH, kinds=None):
    kinds = kinds or ['lru' if l % 2 == 0 else 'ssd' for l in range(depth)]
    jidx = [sum(1 for q in kinds[:l] if q == kinds[l]) for l in range(depth)]
    nc = bass.Bass("TRN2", target_bir_lowering=False)

    def din(name, shape, dt=F32):
        return nc.dram_tensor(name, shape, dt, kind="ExternalInput").ap()

    def dout(name, shape, dt=F32):
        return nc.dram_tensor(name, shape, dt, kind="ExternalOutput").ap()

    def dscr(name, shape, dt=F32):
        return nc.dram_tensor(name, shape, dt).ap()

    xin = din("xin", [128, 8, NTOK])
    pvec_d = din("pvec", [128, NPV])
    prow_d = din("prow", [128, NPR])
    cons_d = din("cons", [128, NCO])
    mvec_d = din("mvec", [128, 8])
    st_lc = din("st_lc", [128, NA, 8, 4, 3])
    st_lh = din("st_lh", [128, NA, 8, 4])
    st_sc = din("st_sc", [128, NB, 24, 4, 3])
    st_ss = din("st_ss", [NB, 4, 128, 2048])
    W = {
        'lru_w_in': din("lru_w_in", [NA, DM, 2048]),
        'lru_w_r': din("lru_w_r", [NA, 1024, 256]),
        'lru_w_i': din("lru_w_i", [NA, 1024, 256]),
        'lru_w_out': din("lru_w_out", [NA, DM, DM]),
        'ssd_w_in': din("ssd_w_in", [NB, DM, 5152]),
        'ssd_w_out': din("ssd_w_out", [NB, 2048, DM]),
        'ffn_w_gate': din("ffn_w_gate", [DEPTH, DM, DFF]),
        'ffn_w_up': din("ffn_w_up", [DEPTH, DM, DFF]),
        'ffn_w_down': din("ffn_w_down", [DEPTH, DFF, DM]),
    }
    yout = dout("y", [128, 8, NTOK])
    o_lc = dout("o_lc", [128, NA, 8, 5, 3])
    o_lh = dout("o_lh", [128, NA, 8, 5])
    o_sc = dout("o_sc", [128, NB, 24, 5, 3])
    o_ss = dout("o_ss", [NB, 5, 128, 2048])

    WB = {k: dscr(k + "_bf", list(v.shape), BF16) for k, v in W.items()}
    wdt_bf = dscr("wdt_bf", [NB, 128, 256], BF16)
    xsc = dscr("xsc", [128, 8, NTOK])
    ag0_in = dscr("ag0_in", [128, 24])
    ag0_out = dscr("ag0_out", [4 * 128, 24])
    agl_in = dscr("agl_in", [128, 16])
    agl_out = dscr("agl_out", [4 * 128, 16])
    ags_in = dscr("ags_in", [128, 2048])
    ags_out = dscr("ags_out", [4 * 128, 2048])
    agd_in = dscr("agd_in", [128, 32])
    agd_out = dscr("agd_out", [4 * 128, 32])

    S = Sched()
    st = contextlib.ExitStack()
    with st:
        def sb(name, shape, dt=F32):
            return st.enter_context(nc.sbuf_tensor(name, shape, dt))

        def psum(name, shape, dt=F32):
            return st.enter_context(nc.psum_tensor(name, shape, dt))

        XT = [sb("XT%d" % i, [128, 8, NT]) for i in range(2)]
        SLOTS = [sb("SL%d" % i, [128, SLOT], BF16) for i in range(3)]
        WRI = sb("WRI", [128, 4096], BF16)
        HN = sb("HN", [128, 8, NT], BF16)
        HNH = sb("HNH", [128, 8, 4], BF16)
        ACTB = sb("ACTB", [128, 22, NT], BF16)
        MIXS = sb("MIXS", [128, 8, NT])
        SQ = sb("SQ", [128, 8, NT], BF16)
        RS0 = sb("RS0", [128, NT])
        RSTD = sb("RSTD", [128, NT])
        PV = sb("PV", [128, NPV])
        PR = sb("PR", [128, NPR])
        CN = sb("CN", [128, NCO])
        MV = sb("MV", [128, 8])
        OMV = sb("OMV", [128, 8])
        IDB = sb("IDB", [128, 128], BF16)
        ONESB = sb("ONESB", [128, 128], BF16)
        KC = sb("KC", [128, 4])
        ABC = sb("ABC", [128, NB * 32])
        C1 = sb("C1", [128, NA * 8])
        CARRY_L = sb("CARRY_L", [128, 8, 5, 3])
        TAIL_L = sb("TAIL_L", [128, 8, 3])
        HST = sb("HST", [128, 8, 5])
        APROD = sb("APROD", [128, 8])
        HALO_ALL = sb("HALO_ALL", [128, 4, 24])
        XHALO = sb("XHALO", [128, 8, 3])
        AGL = sb("AGL", [128, 4, 16])
        XBC = sb("XBC", [128, 6, NT], BF16)
        CBS = [sb("CBS%d" % i, [128, 520]) for i in range(2)]
        XCS = [sb("XCS%d" % i, [128, NT]) for i in range(2)]
        YY = sb("YY", [128, 4, NT])
        RHSD = sb("RHSD", [64, 512])
        LT = sb("LT", [64, 512])
        MT = sb("MT", [64, 512], BF16)
        XTOK = sb("XTOK", [64, 512], BF16)
        BTOK = sb("BTOK", [64, 128], BF16)
        BTM = sb("BTM", [64, 4, 128], BF16)
        XDTE = sb("XDTE", [64, 512], BF16)
        XDTO = sb("XDTO", [64, 512], BF16)
        XDTD = sb("XDTD", [64, 512], BF16)
        DAX = sb("DAX", [64, 512])
        CBM = sb("CBM", [64, 64])
        EACS = sb("EACS", [128, 256])
        YTMP = sb("YTMP", [128, 256])
        DT = sb("DT", [64, 8, 32])
        DTE = sb("DTE", [64, 8, 32])
        DTO = sb("DTO", [64, 8, 32])
        DA = sb("DA", [64, 8, 32])
        DEC = sb("DEC", [64, 8, 32])
        DTDEC = sb("DTDEC", [64, 8, 32])
        BD = sb("BD", [128, 8, 32])
        BDTOT = sb("BDTOT", [128, 32])
        SP = sb("SP", [128, 4, 512])
        SBF = sb("SBF", [128, 4, 512], BF16)
        CARRY_S = sb("CARRY_S", [128, 24, 5, 3])
        STG_S = sb("STG_S", [128, 24, 4, 3])
        TAIL_S = sb("TAIL_S", [128, 24, 3])
        AGT = sb("AGT", [128, 2080])
        COEF = sb("COEF", [128, 32])

        TMPF = XCS
        ZG = SQ[:, 4:8, :]
        XC = YY[:, 0:2, :]
        GG = YY[:, 2:4, :]
        AA = SP[:, 0:2, :]
        UU = SP[:, 2:4, :]
        SS_ = AGT[:, 0:1024].rearrange("p (m t) -> p m t", m=2)
        HH = AGT[:, 1040:2064].rearrange("p (m t) -> p m t", m=2)
        XCB = SBF[:, 0:2, :]

        MXA = MIXS[:].rearrange("p c t -> p (c t)")
        LXC = [XC, MXA[:, 0:1024].rearrange("p (m t) -> p m t", m=2)]
        LGG = [GG, MXA[:, 1024:2048].rearrange("p (m t) -> p m t", m=2)]
        LCB = [[CBS[0][:, :], CBS[1][:, :]], [MXA[:, 2048:2568], MXA[:, 2568:3088]]]
        LXCB = [XCB, MXA[:, 3200:3712].bitcast(BF16).rearrange("p (m t) -> p m t", m=2)]
        MXF = MIXS[:, 0:5, :].rearrange("p c t -> p (c t)")
        MXB = MIXS[:, 5:8, :].rearrange("p c t -> p (c t)").bitcast(BF16)
        TS = [
            {'RHSD': RHSD[:, :], 'LT': LT[:, :], 'DAX': DAX[:, :], 'CBM': CBM[:, :], 'EACS': EACS[:, :],
             'YTMP': YTMP[:, :], 'MT': MT[:, :], 'XTOK': XTOK[:, :], 'BTOK': BTOK[:, :], 'XDTE': XDTE[:, :],
             'XDTO': XDTO[:, :], 'XDTD': XDTD[:, :]},
            {'RHSD': MXF[0:64, 0:512], 'LT': MXF[0:64, 512:1024], 'DAX': MXF[0:64, 1024:1536],
             'CBM': MXF[0:64, 1536:1600], 'EACS': MXF[:, 1664:1920], 'YTMP': MXF[:, 1920:2176],
             'MT': MXB[0:64, 0:512], 'XTOK': MXB[0:64, 512:1024], 'BTOK': MXB[0:64, 1024:1152],
             'XDTE': MXB[0:64, 1536:2048], 'XDTO': MXB[0:64, 2048:2560], 'XDTD': MXB[0:64, 2560:3072]},
        ]

        PBANK = [psum("PB%d" % i, [128, 512]) for i in range(6)]
        PTB = [psum("PT%d" % i, [128, 1024], BF16) for i in range(2)]
        pstate = {'pb': 0, 'pt': 0, 'slot': 0, 'xt': 0, 'cast': 0}

        def pb():
            i = pstate['pb']
            pstate['pb'] = (i + 1) % len(PBANK)
            return PBANK[i], ('pb', i)

        def ptb():
            i = pstate['pt']
            pstate['pt'] = (i + 1) % len(PTB)
            return PTB[i], ('pt', i)

        def V(fn, r, w):
            S.add('vector', fn, r, w)

        def A(fn, r, w):
            S.add('scalar', fn, r, w)

        def P(fn, r, w):
            S.add('tensor', fn, r, w)

        def DM_(fn, r, w, q='sync'):
            S.add(q, fn, r, w, dma=True)

        def vcopy(o, i, r, w):
            V(lambda e, o=o, i=i: e.tensor_copy(out=o, in_=i), r, w)

        def acopy(o, i, r, w):
            A(lambda e, o=o, i=i: e.copy(out=o, in_=i), r, w)

        def act(o, i, f, r, w, bias=None, scale=None):
            kw = {}
            if bias is not None:
                kw['bias'] = bias
            if scale is not None:
                kw['scale'] = scale
            A(lambda e, o=o, i=i, f=f, kw=kw: e.activation(out=o, in_=i, func=f, **kw), r, w)

        def tt(o, a, b, op, r, w):
            V(lambda e, o=o, a=a, b=b, op=op: e.tensor_tensor(out=o, in0=a, in1=b, op=op), r, w)

        def stt(o, a, s, b, op0, op1, r, w):
            V(lambda e, o=o, a=a, s=s, b=b, op0=op0, op1=op1:
              e.scalar_tensor_tensor(out=o, in0=a, scalar=s, in1=b, op0=op0, op1=op1), r, w)

        def ts(o, a, s1, s2, op0, op1, r, w):
            V(lambda e, o=o, a=a, s1=s1, s2=s2, op0=op0, op1=op1:
              e.tensor_scalar(out=o, in0=a, scalar1=s1, scalar2=s2, op0=op0, op1=op1), r, w)

        def mm(o, l, rr, start, stop, r, w):
            P(lambda e, o=o, l=l, rr=rr, start=start, stop=stop:
              e.matmul(o, l, rr, start=start, stop=stop), r, w)

        def dma(o, i, r, w, q='sync'):
            DM_(lambda e, o=o, i=i: e.dma_start(out=o, in_=i), r, w, q)

        def pv(name, idx):
            c = PVO[name] + idx
            return PV[:, c:c + 1]

        cast_q = {}

        def queue_cast(tag, name, lyr):
            rows = W[name].shape[1]
            for r0 in range(0, rows, 256):
                r1 = min(rows, r0 + 256)
                cast_q.setdefault(tag, []).append((name, lyr, r0, r1))

        for layer in range(depth):
            j = jidx[layer]
            if kinds[layer] == 'lru':
                for nm in ('lru_w_in', 'lru_w_r', 'lru_w_i', 'lru_w_out'):
                    queue_cast(('mix', layer), nm, j)
            else:
                for nm in ('ssd_w_in', 'ssd_w_out'):
                    queue_cast(('mix', layer), nm, j)
            for nm in ('ffn_w_gate', 'ffn_w_up', 'ffn_w_down'):
                queue_cast(('ffn', layer), nm, layer)

        cast_order = []
        for layer in range(depth):
            if kinds[layer] == 'ssd':
                cast_order.append(('__wdt__', jidx[layer], 0, 0))
            cast_order += cast_q.pop(('mix', layer), [])
            cast_order += cast_q.pop(('ffn', layer), [])

        def pump_casts(n, depth_=1):
            if S.recording:
                return
            for _ in range(n):
                if not cast_order:
                    return
                name, lyr, r0, r1 = cast_order.pop(0)
                if name == '__wdt__':
                    ci_ = pstate['cast']
                    pstate['cast'] = ci_ + 1
                    dma(wdt_bf[lyr].rearrange("p (k m) -> p k m", k=8),
                        W['ssd_w_in'][lyr, :, 5120:5152].rearrange("(k p) m -> p k m", p=128),
                        [('castchain', ci_ - 2)], [('wb', 'wdt', lyr), ('castchain', ci_)], q='gpsimd')
                    continue
                subs = list(range(r0, r1, 16))
                prev = []
                for si, a in enumerate(subs):
                    b_ = min(r1, a + 16)
                    ci_ = pstate['cast']
                    pstate['cast'] = ci_ + 1
                    last = si == len(subs) - 1
                    wkey = ('wb', name, lyr, r0 // 256) if last else ('wbs', name, lyr, a)
                    dma(WB[name][lyr, a:b_, :], W[name][lyr, a:b_, :], [('castchain', ci_ - depth_)] + (prev if last else []),
                        [wkey, ('castchain', ci_)], q='gpsimd')
                    prev.append(wkey)

        def wkeys(name, lyr, r0, r1):
            return [('wb', name, lyr, b) for b in range(r0 // 256, (r1 + 255) // 256)]

        def load_unit(pieces, buf=None, key=None):
            if buf is None:
                i = pstate['slot']
                pstate['slot'] = (i + 1) % len(SLOTS)
                buf = SLOTS[i]
                key = ('slot', i)
            offs = []
            off = 0
            for (name, lyr, r0, nr, c0, ncol) in pieces:
                nk = nr // 128
                dst = buf[:, off:off + nk * ncol].rearrange("p (k m) -> p k m", k=nk)
                src = WB[name][lyr, r0:r0 + nr, c0:c0 + ncol].rearrange("(k p) m -> p k m", p=128)
                for kk_ in wkeys(name, lyr, r0, r0 + nr):
                    assert S.recording or kk_ in S.last_w, kk_
                dma(dst, src, wkeys(name, lyr, r0, r0 + nr), [key])
                offs.append((off, ncol))
                off += nk * ncol
            assert off <= buf.shape[1]

            def w(pi, k, m0, m1):
                o, ncol = offs[pi]
                return buf[:, o + k * ncol + m0:o + k * ncol + m1]
            return w, key

        dma(PV[:], pvec_d, [], ['PV0'])
        gb = PV[:, PVO['lbr']:PVO['lbr'] + 2 * NA * 8]
        ts(gb, gb, 0.5, None, ALU.mult, ALU.bypass, ['PV0'], ['PV'])
        dma(PR[:], prow_d, [], ['PR'])
        dma(CN[:], cons_d, [], ['CN'])
        dma(MV[:], mvec_d, [], ['MV'])
        vcopy(IDB[:], CN[:, 0:128], ['CN'], ['IDB'])
        V(lambda e: e.memset(ONESB[:], 1.0), [], ['ONESB'])
        V(lambda e: e.memset(KC[:, 0:1], EPS), [], ['KC0'])
        V(lambda e: e.memset(KC[:, 1:2], 1.0), [], ['KC'])
        ts(OMV[:], MV[:], -1.0, 1.0, ALU.mult, ALU.add, ['MV'], ['OMV'])
        act(ABC[:], PR[:, PRO['alog']:PRO['alog'] + NB * 32], AF.Exp, ['PR'], ['ABC0'])
        ts(ABC[:], ABC[:], -1.0, None, ALU.mult, ALU.bypass, ['ABC0'], ['ABC'])
        lam = PV[:, PVO['llam']:PVO['llam'] + NA * 8]
        act(C1[:], lam, AF.Exp, ['PV'], ['C1a'], scale=-1.0)
        act(C1[:], C1[:], AF.Ln, ['C1a', 'KC'], ['C1b'], bias=KC[:, 1:2])
        ts(C1[:], C1[:], -4.0, None, ALU.mult, ALU.bypass, ['C1b'], ['C1'])

        V(lambda e: e.memset(KC[:, 2:3], -0.6931471805599453), [], ['KC2'])

        def cm(name, n=64):
            return CN[0:64, CO[name]:CO[name] + n]

        tiles = [(t * NT, NT, 1, NT, 0) for t in range(NPT // NT)] + [(NPT, 64, 4, 16, 1)]
        ptiles = tiles[:-1]

        XSEQ = []
        xst = {'i': 0, 'pref': set()}

        def _issue_x(i):
            src, tile = XSEQ[i]
            col0, N = tile[0], tile[1]
            b = i % 2
            dma(XT[b][:, :, :N], src[:, :, col0:col0 + N], [('xd', src.name, col0)], [('XT', b)])

        def load_x(src, tile):
            if S.recording:
                XSEQ.append((src, tile))
                return XT[0], ('XT', 0)
            i = xst['i']
            assert XSEQ[i][0].name == src.name and XSEQ[i][1] == tile
            if i not in xst['pref']:
                _issue_x(i)
            return XT[i % 2], ('XT', i % 2)

        def prefetch_x():
            if S.recording:
                return
            i = xst['i'] + 1
            if i < len(XSEQ) and i not in xst['pref']:
                xst['pref'].add(i)
                _issue_x(i)

        def end_x():
            if not S.recording:
                xst['i'] += 1

        def store_x(dst, tile, xt, key):
            col0, N = tile[0], tile[1]
            dma(dst[:, :, col0:col0 + N], xt[:, :, :N], [key], [('xd', dst.name, col0)])

        def rstd_from_sq(nch, N, nfeat, sqkeys):
            bank, bk = pb()
            for c in range(nch):
                mm(bank[:, :N], ONESB[:, :], SQ[:, c, :N], c == 0, c == nch - 1, [sqkeys[c], 'ONESB'], [bk])
            act(RS0[:, :N], bank[:, :N], AF.Ln, [bk, 'KC0'], ['RS0'], bias=KC[:, 0:1], scale=1.0 / nfeat)
            act(RSTD[:, :N], RS0[:, :N], AF.Exp, ['RS0'], ['RSTD'], scale=-0.5)

        def prenorm(xap, xkey, N, gname, layer, out=None, okey='HN'):
            out = HN if out is None else out
            A(lambda e, N=N: e.activation(out=SQ[:, :, :N], in_=xap, func=AF.Square), [xkey],
              [('SQ', c) for c in range(8)])
            rstd_from_sq(8, N, DM, [('SQ', c) for c in range(8)])
            for c in range(8):
                stt(out[:, c, :N], xap[:, c, :], pv(gname, layer * 8 + c), RSTD[:, :N], ALU.mult, ALU.mult,
                    [xkey, 'RSTD', 'PV'], [(okey, c)])

        def evac_post(bank, bk, mg, N):
            acopy(MIXS[:, mg, :N], bank[:, :N], [bk], [('MIXS', mg)])
            act(SQ[:, mg, :N], bank[:, :N], AF.Square, [bk], [('SQ', mg)])

        def postnorm_residual(xt, xkey, N, gname, layer):
            rstd_from_sq(8, N, DM, [('SQ', c) for c in range(8)])
            for c in range(8):
                stt(MIXS[:, c, :N], MIXS[:, c, :N], pv(gname, layer * 8 + c), RSTD[:, :N], ALU.mult, ALU.mult,
                    [('MIXS', c), 'RSTD', 'PV'], [('MIXS', c)])
                tt(xt[:, c, :N], xt[:, c, :N], MIXS[:, c, :N], ALU.add, [('MIXS', c), xkey], [xkey])

        def ffn_tile(layer, tile, src, dst, halo_next=None):
            N = tile[1]
            xt, xkey = load_x(src, tile)
            prenorm(xt[:, :, :N], xkey, N, 'nfp', layer)
            hk = [('HN', k) for k in range(8)]
            for u in range(DFF // 256):
                if u == 4:
                    prefetch_x()
                w, wk = load_unit([('ffn_w_gate', layer, 0, DM, u * 256, 256), ('ffn_w_up', layer, 0, DM, u * 256, 256)])
                for m in range(2):
                    bg, bgk = pb()
                    for k in range(8):
                        mm(bg[:, :N], w(0, k, m * 128, m * 128 + 128), HN[:, k, :N], k == 0, k == 7, [wk, hk[k]], [bgk])
                    bu, buk = pb()
                    for k in range(8):
                        mm(bu[:, :N], w(1, k, m * 128, m * 128 + 128), HN[:, k, :N], k == 0, k == 7, [wk, hk[k]], [buk])
                    tf = TMPF[m]
                    act(tf[:, :N], bg[:, :N], AF.Silu, [bgk], [('XCS', m)])
                    tt(ACTB[:, 2 * u + m, :N], tf[:, :N], bu[:, :N], ALU.mult, [('XCS', m), buk], [('ACTB', 2 * u + m)])
            for u in range(4):
                w, wk = load_unit([('ffn_w_down', layer, 0, DFF, u * 256, 256)])
                for m in range(2):
                    bank, bk = pb()
                    for k in range(22):
                        mm(bank[:, :N], w(0, k, m * 128, m * 128 + 128), ACTB[:, k, :N], k == 0, k == 21,
                           [wk, ('ACTB', k)], [bk])
                    evac_post(bank, bk, 2 * u + m, N)
            postnorm_residual(xt, xkey, N, 'nfq', layer)
            store_x(dst, tile, xt, xkey)
            if halo_next is not None:
                halo_exchange(None, halo_next, xt, xkey, N)
            end_x()

        def conv_chunk(bank, bk, cbuf, cbk, xc, xck, carry, ck, N, nseq, L, wname, bname, widx, bidx):
            cb3 = cbuf[:, 0:nseq * (3 + L)].rearrange("p (s t) -> p s t", s=nseq)
            vcopy(cb3[:, :, 0:3], carry, [ck], [cbk])
            acopy(cb3[:, :, 3:3 + L], bank[:, :N].rearrange("p (s t) -> p s t", s=nseq), [bk], [cbk])
            vcopy(carry, cb3[:, :, L:L + 3], [cbk], [ck])
            xc3 = xc.rearrange("p (s t) -> p s t", s=nseq)
            act(xc3, bank[:, :N].rearrange("p (s t) -> p s t", s=nseq), AF.Identity, [bk, 'PV'], [xck],
                bias=pv(bname, bidx), scale=pv(wname, widx(3)))
            for k in range(0, 3):
                stt(xc3, cb3[:, :, k:k + L], pv(wname, widx(k)), xc3, ALU.mult, ALU.add, [cbk, xck, 'PV'], [xck])

        def halo_exchange(src, layer, xt=None, xkey=None, N=None):
            if xt is None:
                dma(ag0_in.rearrange("p (c t) -> p c t", c=8), src[:, :, NPT - 3:NPT], [('xd', src.name, NPT - NT)],
                    ['ag0_in'])
            else:
                vcopy(HALO_ALL[:, 0, :].rearrange("p (c t) -> p c t", c=8), xt[:, :, N - 3:N], [xkey], ['HALO_STG'])
                dma(ag0_in, HALO_ALL[:, 0, :], ['HALO_STG'], ['ag0_in'])
            S.add('gpsimd', lambda e: e.collective_compute("AllGather", ALU.bypass, replica_groups=RG,
                                                           ins=[ag0_in.opt()], outs=[ag0_out.opt()]),
                  ['ag0_in'], ['ag0_out'])
            if layer == 1:
                pump_casts(19)
            dma(HALO_ALL[:], ag0_out.rearrange("(r p) f -> p r f", p=128), ['ag0_out', 'HALO_STG'], ['HALO_ALL'])
            xh = XHALO[:].rearrange("p c t -> p (c t)")
            ts(xh, HALO_ALL[:, 0, :], MV[:, 4:5], None, ALU.mult, ALU.bypass, ['HALO_ALL', 'MV'], ['XHALO'])
            for r in range(1, 4):
                stt(xh, HALO_ALL[:, r, :], MV[:, 4 + r:5 + r], xh, ALU.mult, ALU.add, ['HALO_ALL', 'MV', 'XHALO'],
                    ['XHALO'])
            prenorm(XHALO[:, :, :], 'XHALO', 3, 'nmp', layer, out=HNH, okey='HNH')

        def lru_tile(layer, tile, full, src, dst, first):
            j = jidx[layer]
            col0, N, nseq, L, seq0 = tile
            xt, xkey = load_x(src, tile)
            prenorm(xt[:, :, :N], xkey, N, 'nmp', layer)
            hk = [('HN', k) for k in range(8)]
            wri, wrik = load_unit([('lru_w_r', j, b * 256, 256, 0, 256) for b in range(4)] +
                                  [('lru_w_i', j, b * 256, 256, 0, 256) for b in range(4)], buf=WRI, key='WRI')
            def lru_front(b):
                sx = b % 2
                pieces = [('lru_w_in', j, 0, DM, 1024 + b * 256, 256)]
                if full:
                    pieces.append(('lru_w_in', j, 0, DM, b * 256, 256))
                w, wk = load_unit(pieces)
                if first:
                    for m in range(2):
                        c = 2 * b + m
                        bank, bk = pb()
                        for k in range(8):
                            mm(bank[:, :3], w(0, k, m * 128, m * 128 + 128), HNH[:, k, :3], k == 0, k == 7,
                               [wk, ('HNH', k)], [bk])
                        acopy(TAIL_L[:, c, :], bank[:, :3], [bk], [('TAIL_L', c)])
                        vcopy(CARRY_L[:, c, 0, :], TAIL_L[:, c, :], [('TAIL_L', c)], [('CARRY_L', c)])
                for m in range(2):
                    c = 2 * b + m
                    bank, bk = pb()
                    for k in range(8):
                        mm(bank[:, :N], w(0, k, m * 128, m * 128 + 128), HN[:, k, :N], k == 0, k == 7, [wk, hk[k]], [bk])
                    conv_chunk(bank, bk, LCB[sx][m], ('CBS', sx, m), LXC[sx][:, m, :N], ('XC', sx, m),
                               CARRY_L[:, c, seq0:seq0 + nseq, :], ('CARRY_L', c), N, nseq, L,
                               'lcw', 'lcb', lambda k, c=c: (j * 4 + k) * 8 + c, j * 8 + c)
                if full:
                    for m in range(2):
                        bank, bk = pb()
                        for k in range(8):
                            mm(bank[:, :N], w(1, k, m * 128, m * 128 + 128), HN[:, k, :N], k == 0, k == 7, [wk, hk[k]], [bk])
                        act(LGG[sx][:, m, :N], bank[:, :N], AF.Gelu_apprx_tanh, [bk], [('GG', sx, m)])
                for m in range(2):
                    acopy(LXCB[sx][:, m, :N], LXC[sx][:, m, :N], [('XC', sx, m)], [('XCB', sx, m)])
            def lru_back(b):
                sx = b % 2
                banks = []
                for mo in range(2):
                    br, brk = pb()
                    for kk in range(2):
                        mm(br[:, :N], wri(b, kk, mo * 128, mo * 128 + 128), LXCB[sx][:, kk, :N], kk == 0, kk == 1,
                           [wrik, ('XCB', sx, kk)], [brk])
                    bi, bik = pb()
                    for kk in range(2):
                        mm(bi[:, :N], wri(4 + b, kk, mo * 128, mo * 128 + 128), LXCB[sx][:, kk, :N], kk == 0, kk == 1,
                           [wrik, ('XCB', sx, kk)], [bik])
                    banks.append((br, brk, bi, bik))
                for mo in range(2):
                    c = 2 * b + mo
                    br, brk, bi, bik = banks[mo]
                    act(AA[:, mo, :N], br[:, :N], AF.Tanh, [brk, 'PV'], [('AA', mo)],
                        bias=pv('lbr', j * 8 + c), scale=0.5)
                    act(UU[:, mo, :N], bi[:, :N], AF.Tanh, [bik, 'PV'], [('UU', mo)],
                        bias=pv('lbi', j * 8 + c), scale=0.5)
                for mo in range(2):
                    c = 2 * b + mo
                    act(AA[:, mo, :N], AA[:, mo, :N], AF.Exp, [('AA', mo), 'C1'], [('AA', mo)],
                        scale=C1[:, j * 8 + c:j * 8 + c + 1], bias=C1[:, j * 8 + c:j * 8 + c + 1])
                    tt(SS_[:, mo, :N], AA[:, mo, :N], AA[:, mo, :N], ALU.mult, [('AA', mo)], [('SSQ', mo)])
                for mo in range(2):
                    act(SS_[:, mo, :N], SS_[:, mo, :N], AF.Ln, [('SSQ', mo), 'KC'], [('SSQ', mo)], bias=KC[:, 1:2],
                        scale=-1.0)
                    act(SS_[:, mo, :N], SS_[:, mo, :N], AF.Exp, [('SSQ', mo), 'KC2'], [('SSQ', mo)], scale=0.5,
                        bias=KC[:, 2:3])
                for mo in range(2):
                    c = 2 * b + mo
                    stt(UU[:, mo, :N], UU[:, mo, :N], 1.0, LXC[sx][:, mo, :N], ALU.add, ALU.mult,
                        [('UU', mo), ('XC', sx, mo)], [('UU', mo)])
                    tt(UU[:, mo, :N], UU[:, mo, :N], SS_[:, mo, :N], ALU.mult, [('UU', mo), ('SSQ', mo)], [('UU', mo)])
                    for s_ in range(nseq):
                        sid = seq0 + s_
                        sl = slice(s_ * L, (s_ + 1) * L)
                        V(lambda e, mo=mo, sl=sl, c=c, sid=sid: e.tensor_tensor_scan(
                            out=HH[:, mo, sl], data0=AA[:, mo, sl], data1=UU[:, mo, sl],
                            initial=HST[:, c, sid:sid + 1], op0=ALU.mult, op1=ALU.add),
                          [('AA', mo), ('UU', mo), ('HST', c)], [('HH', mo)])
                        vcopy(HST[:, c, sid:sid + 1], HH[:, mo, (s_ + 1) * L - 1:(s_ + 1) * L], [('HH', mo)], [('HST', c)])
                    if not full:
                        V(lambda e, mo=mo, c=c, N=N: e.tensor_tensor_scan(
                            out=SS_[:, mo, :N], data0=AA[:, mo, :N], data1=TMPF[1][:, :N],
                            initial=APROD[:, c:c + 1], op0=ALU.mult, op1=ALU.add),
                          [('AA', mo), ('XCS', 1), ('APROD', c), ('SSQ', mo)], [('SSQ', mo)])
                        vcopy(APROD[:, c:c + 1], SS_[:, mo, N - 1:N], [('SSQ', mo)], [('APROD', c)])
                    if full:
                        tt(ACTB[:, c, :N], HH[:, mo, :N], LGG[sx][:, mo, :N], ALU.mult, [('HH', mo), ('GG', sx, mo)],
                           [('ACTB', c)])
            lru_front(0)
            for b in range(4):
                if b + 1 < 4:
                    lru_front(b + 1)
                lru_back(b)
                if b == 1:
                    prefetch_x()
            if not full:
                end_x()
                return
            for u in range(2):
                w, wk = load_unit([('lru_w_out', j, 0, DM, u * 512, 512)])
                for m in range(4):
                    bank, bk = pb()
                    for k in range(8):
                        mm(bank[:, :N], w(0, k, m * 128, m * 128 + 128), ACTB[:, k, :N], k == 0, k == 7,
                           [wk, ('ACTB', k)], [bk])
                    evac_post(bank, bk, 4 * u + m, N)
            postnorm_residual(xt, xkey, N, 'nmq', layer)
            store_x(dst, tile, xt, xkey)
            end_x()

        def lru_layer(layer, src, dst):
            j = jidx[layer]
            hk_all = [('HST', c) for c in range(8)]
            ck_all = [('CARRY_L', c) for c in range(8)]
            V(lambda e: e.memset(HST[:, :, 0:1], 0.0), [], hk_all)
            V(lambda e: e.memset(APROD[:], 1.0), [], [('APROD', c) for c in range(8)])
            V(lambda e: e.memset(TMPF[1][:, :], 0.0), [], [('XCS', 1)])
            for ti, tile in enumerate(ptiles):
                lru_tile(layer, tile, False, src, dst, ti == 0)
            vcopy(AGL[:, 0, 0:8], HST[:, :, 0], hk_all, ['AGLs'])
            vcopy(AGL[:, 0, 8:16], APROD[:], [('APROD', c) for c in range(8)], ['AGLs'])
            dma(agl_in, AGL[:, 0, :], ['AGLs'], ['agl_in'])
            S.add('gpsimd', lambda e: e.collective_compute("AllGather", ALU.bypass, replica_groups=RG,
                                                           ins=[agl_in.opt()], outs=[agl_out.opt()]),
                  ['agl_in'], ['agl_out'])
            pump_casts(31 if layer == 0 else 10000)
            stg = STG_S[:].rearrange("p a b c -> p (a b c)")
            dma(stg[:, 0:96], st_lc[:, j, :, :, :].rearrange("p a b c -> p (a b c)"), [], ['STG_S'])
            dma(stg[:, 96:128], st_lh[:, j, :, :].rearrange("p a b -> p (a b)"), [], ['STG_S2'])
            vcopy(CARRY_L[:, :, 1:5, :], stg[:, 0:96].rearrange("p (a b c) -> p a b c", a=8, b=4), ['STG_S'], ck_all)
            vcopy(HST[:, :, 1:5], stg[:, 96:128].rearrange("p (a b) -> p a b", a=8), ['STG_S2'], hk_all)
            lru_tile(layer, tiles[-1], True, src, dst, False)
            dma(AGL[:], agl_out.rearrange("(r p) f -> p r f", p=128), ['agl_out', 'AGLs'], ['AGL'])
            V(lambda e: e.memset(HST[:, :, 0:1], 0.0), ['AGLs'], hk_all)
            h0 = HST[:, :, 0]
            for r in range(4):
                ts(COEF[:, 0:8], AGL[:, r, 8:16], MV[:, r:r + 1], OMV[:, r:r + 1], ALU.mult, ALU.add,
                   ['AGL', 'MV', 'OMV'], ['COEF'])
                tt(h0, h0, COEF[:, 0:8], ALU.mult, hk_all + ['COEF'], hk_all)
                stt(h0, AGL[:, r, 0:8], MV[:, r:r + 1], h0, ALU.mult, ALU.add, ['AGL', 'MV'] + hk_all, hk_all)
            vcopy(CARRY_L[:, :, 0, :], TAIL_L[:], [('TAIL_L', c) for c in range(8)], ck_all)
            for tile in ptiles:
                lru_tile(layer, tile, True, src, dst, False)
            dma(o_lh[:, j, :, :], HST[:], hk_all, [('o_lh', j)])
            dma(o_lc[:, j, :, :, :], CARRY_L[:], ck_all, [('o_lc', j)])

        def ssd_tile(layer, tile, full, src, dst, first):
            j = jidx[layer]
            col0, N, nseq, L, seq0 = tile
            sample = nseq > 1
            nch = 1 if sample else N // 64
            nsub = 4 if sample else 1
            Um, Tm, CMm = (cm('U16'), cm('T16'), cm('CM16')) if sample else (cm('U64'), cm('T64'), cm('CM64'))
            xt, xkey = load_x(src, tile)
            prenorm(xt[:, :, :N], xkey, N, 'nmp', layer)
            hk = [('HN', k) for k in range(8)]
            si_ = pstate['slot']
            pstate['slot'] = (si_ + 1) % len(SLOTS)
            wdk = ('slot', si_)
            assert S.recording or ('wb', 'wdt', j) in S.last_w
            dma(SLOTS[si_][:, 0:256], wdt_bf[j], [('wb', 'wdt', j)], [wdk])

            def wdt(pi, k, m0, m1, _b=SLOTS[si_]):
                return _b[:, k * 32 + m0:k * 32 + m1]
            for cc in range(nch):
                bank, bk = pb()
                for k in range(8):
                    mm(bank[:64, :32], HN[:, k, cc * 64:cc * 64 + 64], wdt(0, k, 0, 32), k == 0, k == 7, [wdk, hk[k]], [bk])
                tt(DT[:, cc, :], bank[:64, :32], PR[0:64, PRO['dtb'] + j * 32:PRO['dtb'] + j * 32 + 32], ALU.add,
                   [bk, 'PR'], ['DT'])
            dtv = DT[:, 0:nch, :]
            act(dtv, dtv, AF.Exp, ['DT'], ['DT'])
            act(dtv, dtv, AF.Ln, ['DT', 'KC'], ['DT'], bias=KC[0:64, 1:2])
            abc = ABC[0:64, j * 32:j * 32 + 32].unsqueeze(1).broadcast_to([64, nch, 32])
            tt(DA[:, 0:nch, :], dtv, abc, ALU.mult, ['DT', 'ABC'], ['DA'])
            if full:
                ev = PR[0:64, PRO['even']:PRO['even'] + 32].unsqueeze(1).broadcast_to([64, nch, 32])
                od = PR[0:64, PRO['odd']:PRO['odd'] + 32].unsqueeze(1).broadcast_to([64, nch, 32])
                tt(DTE[:, 0:nch, :], dtv, ev, ALU.mult, ['DT', 'PR'], ['DTE'])
                tt(DTO[:, 0:nch, :], dtv, od, ALU.mult, ['DT', 'PR'], ['DTO'])
            for cc in range(nch):
                bank, bk = pb()
                mm(bank[:64, :32], Um, DA[:, cc, :], True, True, ['CN', 'DA'], [bk])
                act(DEC[:, cc, :], bank[:64, :32], AF.Exp, [bk], ['DEC'])
                for q in range(nsub):
                    bank, bk = pb()
                    mi = (1 + q) if sample else 0
                    mrep = CN[0:64, CO['MREP'] + mi * 128:CO['MREP'] + mi * 128 + 128]
                    mm(bank[:, :32], mrep, DA[:, cc, :], True, True, ['CN', 'DA'], [bk])
                    act(BD[:, cc * nsub + q, :], bank[:, :32], AF.Exp, [bk], ['BD'])
                    if not full:
                        tt(BDTOT[:], BDTOT[:], BD[:, cc * nsub + q, :], ALU.mult, ['BD', 'BDTOT'], ['BDTOT'])
            tt(DTDEC[:, 0:nch, :], dtv, DEC[:, 0:nch, :], ALU.mult, ['DT', 'DEC'], ['DTDEC'])

            for g in range(4):
                hs = slice(8 * g, 8 * g + 8)
                if g == 2:
                    prefetch_x()
                wz = wzk = None
                if full:
                    wz, wzk = load_unit([('ssd_w_in', j, 0, DM, 512 * g, 512), ('ssd_w_in', j, 0, DM, 4608 + 128 * g, 128)])
                    for m in range(4):
                        bank, bk = pb()
                        for k in range(8):
                            mm(bank[:, :N], wz(0, k, m * 128, m * 128 + 128), HN[:, k, :N], k == 0, k == 7, [wzk, hk[k]], [bk])
                        act(ZG[:, m, :N], bank[:, :N], AF.Silu, [bk], [('SQ', 4 + m)])
                elif first:
                    wz, wzk = load_unit([('ssd_w_in', j, 0, DM, 4608 + 128 * g, 128)])
                w, wk = load_unit([('ssd_w_in', j, 0, DM, 2048 + 512 * g, 512), ('ssd_w_in', j, 0, DM, 4096 + 128 * g, 128)])
                chunks = [(w, wk, 0, m, 4 * g + m) for m in range(4)] + [(w, wk, 1, 0, 16 + g)]
                if full:
                    chunks.append((wz, wzk, 1, 0, 20 + g))
                elif first:
                    chunks.append((wz, wzk, 0, 0, 20 + g))
                pend = None
                for ci, (wa, wak, pi, m, cx) in enumerate(chunks):
                    if first:
                        bank, bk = pb()
                        for k in range(8):
                            mm(bank[:, :3], wa(pi, k, m * 128, m * 128 + 128), HNH[:, k, :3], k == 0, k == 7,
                               [wak, ('HNH', k)], [bk])
                        acopy(TAIL_S[:, cx, :], bank[:, :3], [bk], [('TAIL_S', cx)])
                        vcopy(CARRY_S[:, cx, 0, :], TAIL_S[:, cx, :], [('TAIL_S', cx)], [('CARRY_S', cx)])
                    if ci == 5 and not full:
                        continue
                    bank, bk = pb()
                    for k in range(8):
                        mm(bank[:, :N], wa(pi, k, m * 128, m * 128 + 128), HN[:, k, :N], k == 0, k == 7, [wak, hk[k]], [bk])
                    r = ci % 2
                    conv_chunk(bank, bk, CBS[r][:, :], ('CBS', 0, r), XCS[r][:, :N], ('XCS', r),
                               CARRY_S[:, cx, seq0:seq0 + nseq, :], ('CARRY_S', cx), N, nseq, L,
                               'scw', 'scb', lambda k, cx=cx: (j * 4 + k) * 24 + cx, j * 24 + cx)
                    if pend is not None:
                        act(XBC[:, pend[0], :N], XCS[pend[1]][:, :N], AF.Silu, [('XCS', pend[1])], [('XBC', pend[0])])
                    pend = (ci, r)
                if pend is not None:
                    act(XBC[:, pend[0], :N], XCS[pend[1]][:, :N], AF.Silu, [('XCS', pend[1])], [('XBC', pend[0])])
                if sample:
                    skeys = [('SP', q) for q in range(4)]
                    for q in range(4):
                        dma(SP[:, q, :], st_ss[j, q, :, 512 * g:512 * g + 512], [], [skeys[q]])
                    Sst = [SP[:, q, :] for q in range(4)]
                else:
                    skeys = [('SP', g)]
                    Sst = [SP[:, g, :]]
                Lq = 64 // nsub

                def h8(t):
                    return t.rearrange("p (h q) -> p h q", h=8)

                def b3(t, cc):
                    return t[:, cc, hs].unsqueeze(2).broadcast_to([64, 8, 64])

                def stage_a1a(cc):
                    T_ = TS[cc % 2]
                    k_ = lambda n: (n, cc % 2)
                    cs = slice(cc * 64, cc * 64 + 64)
                    hd = {}
                    if full:
                        tt(h8(T_['RHSD']), b3(DA, cc), Tm.unsqueeze(1).broadcast_to([64, 8, 64]), ALU.mult,
                           ['DA', 'CN'], [k_('RHSD')])
                        vcopy(h8(T_['DAX']), b3(DA, cc), ['DA'], [k_('DAX')])
                    tb, tbk = ptb()
                    for m in range(4):
                        P(lambda e, tb=tb, m=m, cs=cs: e.transpose(tb[0:64, m * 128:m * 128 + 128], XBC[:, m, cs], IDB[:]),
                          [('XBC', m), 'IDB'], [tbk])
                    P(lambda e, tb=tb, cs=cs: e.transpose(tb[0:64, 512:640], XBC[:, 4, cs], IDB[:]),
                      [('XBC', 4), 'IDB'], [tbk])
                    hd['tb'] = (tb, tbk)
                    return hd

                def stage_a1b(cc, hd):
                    T_ = TS[cc % 2]
                    k_ = lambda n: (n, cc % 2)
                    tb, tbk = hd['tb']
                    acopy(T_['XTOK'], tb[0:64, 0:512], [tbk], [k_('XTOK')])
                    acopy(T_['BTOK'], tb[0:64, 512:640], [tbk], [k_('BTOK')])
                    x3 = h8(T_['XTOK'])
                    tt(h8(T_['XDTD']), x3, b3(DTDEC, cc), ALU.mult, [k_('XTOK'), 'DTDEC'], [k_('XDTD')])
                    if full:
                        tt(h8(T_['XDTE']), x3, b3(DTE, cc), ALU.mult, [k_('XTOK'), 'DTE'], [k_('XDTE')])
                        tt(h8(T_['XDTO']), x3, b3(DTO, cc), ALU.mult, [k_('XTOK'), 'DTO'], [k_('XDTO')])
                        cs = slice(cc * 64, cc * 64 + 64)
                        bank, bk = pb()
                        mm(bank[:64, :64], XBC[:, 4, cs], XBC[:, 5, cs], True, True, [('XBC', 4), ('XBC', 5)], [bk])
                        hd['cb'] = (bank, bk)
                        bank, bk = pb()
                        mm(bank[:64, :512], Um, T_['RHSD'], True, True, ['CN', k_('RHSD')], [bk])
                        hd['D'] = (bank, bk)
                        bank, bk = pb()
                        for fc in range(4):
                            mm(bank[:, fc * 64:fc * 64 + 64], T_['DAX'][:, fc * 128:fc * 128 + 128], Tm, True, True,
                               [k_('DAX'), 'CN'], [bk])
                        hd['acs'] = (bank, bk)

                def stage_a2(cc, hd):
                    T_ = TS[cc % 2]
                    k_ = lambda n: (n, cc % 2)
                    bank, bk = hd['cb']
                    tt(T_['CBM'], bank[:64, :64], CMm, ALU.mult, [bk, 'CN'], [k_('CBM')])
                    bank, bk = hd['D']
                    act(T_['LT'], bank[:64, :512], AF.Exp, [bk], [k_('LT')])
                    bank, bk = hd['acs']
                    act(T_['EACS'], bank[:, :256], AF.Exp, [bk], [k_('EACS')])
                    tt(h8(T_['MT']), h8(T_['LT']), T_['CBM'].unsqueeze(1).broadcast_to([64, 8, 64]), ALU.mult,
                       [k_('LT'), k_('CBM')], [k_('MT')])
                    bd_, bdk = pb()
                    for fc in range(4):
                        mm(bd_[:, fc * 64:fc * 64 + 64], T_['XDTE'][:, fc * 128:fc * 128 + 128],
                           T_['MT'][:, (2 * fc) * 64:(2 * fc) * 64 + 64], True, False, [k_('XDTE'), k_('MT')], [bdk])
                        mm(bd_[:, fc * 64:fc * 64 + 64], T_['XDTO'][:, fc * 128:fc * 128 + 128],
                           T_['MT'][:, (2 * fc + 1) * 64:(2 * fc + 1) * 64 + 64], False, True, [k_('XDTO'), k_('MT')], [bdk])
                    hd['yd'] = (bd_, bdk)

                def stage_b(cc, hd):
                    T_ = TS[cc % 2]
                    k_ = lambda n: (n, cc % 2)
                    cs = slice(cc * 64, cc * 64 + 64)
                    if full:
                        bo, bok = pb()
                        for q in range(nsub):
                            for fc in range(4):
                                mm(bo[:, fc * 64 + q * Lq:fc * 64 + (q + 1) * Lq], SBF[:, q, fc * 128:fc * 128 + 128],
                                   XBC[:, 5, cc * 64 + q * Lq:cc * 64 + (q + 1) * Lq], True, True,
                                   [('SBF', q), ('XBC', 5)], [bok])
                    sbanks = []
                    for q in range(nsub):
                        if sample:
                            ts(BTM[:, q, :], T_['BTOK'], CN[0:64, CO['SUBM'] + q:CO['SUBM'] + q + 1], None, ALU.mult,
                               ALU.bypass, [k_('BTOK'), 'CN'], [('BTM', q)])
                            lhs, lk = BTM[:, q, :], ('BTM', q)
                        else:
                            lhs, lk = T_['BTOK'], k_('BTOK')
                        bank, bk = pb()
                        mm(bank[:, :512], lhs, T_['XDTD'], True, True, [lk, k_('XDTD')], [bk])
                        sbanks.append((bank, bk))
                    if full:
                        bd_, bdk = hd['yd']
                        tt(T_['YTMP'], bo[:, :256], T_['EACS'], ALU.mult, [bok, k_('EACS')], [k_('YTMP')])
                        tt(YY[:, :, cs], T_['YTMP'].rearrange("p (f q) -> p f q", f=4),
                           bd_[:, :256].rearrange("p (f q) -> p f q", f=4), ALU.add, [k_('YTMP'), bdk], ['YY'])
                    for q in range(nsub):
                        bank, bk = sbanks[q]
                        s3 = Sst[q].rearrange("p (h q) -> p h q", h=8)
                        tt(s3, s3, BD[:, cc * nsub + q, hs].unsqueeze(2).broadcast_to([128, 8, 64]), ALU.mult,
                           [skeys[q], 'BD', ('SBF', q)], [skeys[q]])
                        tt(Sst[q], Sst[q], bank[:, :512], ALU.add, [skeys[q], bk], [skeys[q]])
                    if full and cc + 1 < nch:
                        for q in range(nsub):
                            acopy(SBF[:, q, :], Sst[q], [skeys[q]], [('SBF', q)])

                if full:
                    for q in range(nsub):
                        acopy(SBF[:, q, :], Sst[q], [skeys[q]], [('SBF', q)])
                hd_cur = stage_a1a(0)
                stage_a1b(0, hd_cur)
                for cc in range(nch):
                    hd_next = stage_a1a(cc + 1) if cc + 1 < nch else None
                    if full:
                        stage_a2(cc, hd_cur)
                    if hd_next is not None:
                        stage_a1b(cc + 1, hd_next)
                    stage_b(cc, hd_cur)
                    hd_cur = hd_next
                if sample:
                    for q in range(4):
                        dma(o_ss[j, 1 + q, :, 512 * g:512 * g + 512], SP[:, q, :], [skeys[q]], [('o_ss', j, 1 + q, g)])
                if not full:
                    continue
                for fc in range(4):
                    stt(YY[:, fc, :N], XBC[:, fc, :N], pv('sdx', j * 16 + 4 * g + fc), YY[:, fc, :N], ALU.mult, ALU.add,
                        [('XBC', fc), 'YY', 'PV'], ['YY'])
                tt(YY[:, :, :N], YY[:, :, :N], ZG[:, :, :N], ALU.mult, ['YY'] + [('SQ', 4 + m) for m in range(4)], ['YY'])
                A(lambda e, N=N: e.activation(out=SQ[:, 0:4, :N], in_=YY[:, :, :N], func=AF.Square), ['YY'],
                  [('SQ', c) for c in range(4)])
                rstd_from_sq(4, N, 512, [('SQ', c) for c in range(4)])
                for fc in range(4):
                    stt(ACTB[:, 4 * g + fc, :N], YY[:, fc, :N], pv('sgn', j * 16 + 4 * g + fc), RSTD[:, :N], ALU.mult,
                        ALU.mult, ['YY', 'RSTD', 'PV'], [('ACTB', 4 * g + fc)])
            if not full:
                end_x()
                return
            for u in range(4):
                w, wk = load_unit([('ssd_w_out', j, 0, 2048, u * 256, 256)])
                for m in range(2):
                    bank, bk = pb()
                    for k in range(16):
                        mm(bank[:, :N], w(0, k, m * 128, m * 128 + 128), ACTB[:, k, :N], k == 0, k == 15,
                           [wk, ('ACTB', k)], [bk])
                    evac_post(bank, bk, 2 * u + m, N)
            postnorm_residual(xt, xkey, N, 'nmq', layer)
            store_x(dst, tile, xt, xkey)
            end_x()

        def ssd_layer(layer, src, dst):
            j = jidx[layer]
            spk = [('SP', g) for g in range(4)]
            ck_all = [('CARRY_S', c) for c in range(24)]
            V(lambda e: e.memset(SP[:], 0.0), [], spk)
            V(lambda e: e.memset(BDTOT[:], 1.0), [], ['BDTOT'])
            for ti, tile in enumerate(ptiles):
                ssd_tile(layer, tile, False, src, dst, ti == 0)
            dma(ags_in, SP[:].rearrange("p g f -> p (g f)"), spk, ['ags_in'])
            dma(agd_in, BDTOT[:], ['BDTOT'], ['agd_in'])
            S.add('gpsimd', lambda e: e.collective_compute("AllGather", ALU.bypass, replica_groups=RG,
                                                           ins=[ags_in.opt()], outs=[ags_out.opt()]),
                  ['ags_in'], ['ags_out'])
            S.add('gpsimd', lambda e: e.collective_compute("AllGather", ALU.bypass, replica_groups=RG,
                                                           ins=[agd_in.opt()], outs=[agd_out.opt()]),
                  ['agd_in', 'ags_out'], ['agd_out'])
            pump_casts(31 if layer == 0 else 10000)
            dma(STG_S[:], st_sc[:, j, :, :, :], [], ['STG_S'])
            vcopy(CARRY_S[:, :, 1:5, :], STG_S[:], ['STG_S'], ck_all)
            ssd_tile(layer, tiles[-1], True, src, dst, False)
            V(lambda e: e.memset(SP[:], 0.0), ['ags_in'], spk)
            for r in range(4):
                dma(AGT[:, 0:2048], ags_out[r * 128:(r + 1) * 128, :], ['ags_out'], ['AGT'])
                dma(AGT[:, 2048:2080], agd_out[r * 128:(r + 1) * 128, :], ['agd_out'], ['AGTd'])
                ts(COEF[:], AGT[:, 2048:2080], MV[:, r:r + 1], OMV[:, r:r + 1], ALU.mult, ALU.add,
                   ['AGTd', 'MV', 'OMV'], ['COEF'])
                for g in range(4):
                    s3 = SP[:, g, :].rearrange("p (h q) -> p h q", h=8)
                    tt(s3, s3, COEF[:, 8 * g:8 * g + 8].unsqueeze(2).broadcast_to([128, 8, 64]), ALU.mult,
                       [spk[g], 'COEF'], [spk[g]])
                    stt(SP[:, g, :], AGT[:, 512 * g:512 * g + 512], MV[:, r:r + 1], SP[:, g, :], ALU.mult, ALU.add,
                        ['AGT', 'MV', spk[g]], [spk[g]])
            vcopy(CARRY_S[:, :, 0, :], TAIL_S[:], [('TAIL_S', c) for c in range(24)], ck_all)
            for tile in ptiles:
                ssd_tile(layer, tile, True, src, dst, False)
            dma(o_ss[j, 0, :, :], SP[:].rearrange("p g f -> p (g f)"), spk, [('o_ss', j, 0)])
            dma(o_sc[:, j, :, :, :], CARRY_S[:], ck_all, [('o_sc', j)])

        def emit_all():
            halo_exchange(xin, 0)
            pump_casts(16, 8)
            for layer in range(depth):
                src = xin if layer == 0 else xsc
                if kinds[layer] == 'lru':
                    lru_layer(layer, src, xsc)
                else:
                    ssd_layer(layer, src, xsc)
                fdst = yout if layer == depth - 1 else xsc
                for ti, tile in enumerate(ptiles):
                    hn_ = layer + 1 if (ti == len(ptiles) - 1 and layer + 1 < depth) else None
                    ffn_tile(layer, tile, xsc, fdst, halo_next=hn_)
                ffn_tile(layer, tiles[-1], xsc, fdst)

        S.recording = True
        emit_all()
        S.recording = False
        for k_ in pstate:
            pstate[k_] = 0
        emit_all()
        assert xst['i'] == len(XSEQ)
        pump_casts(10000)
        S.emit(nc, st)
    return nc, S


def _fm(v):
    v = np.asarray(v, np.float32)
    nch = v.shape[-1] // 128
    v = v.reshape(v.shape[:-1] + (nch, 128))
    return np.ascontiguousarray(np.moveaxis(v, -1, 0))


def _consts():
    c = np.zeros((128, NCO), np.float32)
    c[:, 0:128] = np.eye(128, dtype=np.float32)
    k = np.arange(64)[:, None]
    i = np.arange(64)[None, :]
    c[0:64, CO['U64']:CO['U64'] + 64] = (k > i)
    c[0:64, CO['T64']:CO['T64'] + 64] = (k <= i)
    c[0:64, CO['CM64']:CO['CM64'] + 64] = (k <= i)
    same = (k // 16) == (i // 16)
    c[0:64, CO['U16']:CO['U16'] + 64] = (k > i) & same
    c[0:64, CO['T16']:CO['T16'] + 64] = (k <= i) & same
    c[0:64, CO['CM16']:CO['CM16'] + 64] = (k <= i) & same
    c[0:64, CO['MREP']:CO['MREP'] + 128] = 1.0
    for q in range(4):
        c[16 * q:16 * q + 16, CO['MREP'] + (1 + q) * 128:CO['MREP'] + (2 + q) * 128] = 1.0
        c[16 * q:16 * q + 16, CO['SUBM'] + q] = 1.0
    return c


_CACHE = {}
RUNNER = None


def kernel(**inp):
    return _run(inp, DEPTH)


def _run(inp, depth, kinds=None):
    inp = {k: np.asarray(v) for k, v in inp.items()}
    ck = (depth, tuple(kinds) if kinds else None)
    if ck not in _CACHE:
        _CACHE[ck] = build_program(depth, kinds)[0]
    nc = _CACHE[ck]
    f32 = np.float32
    xp = inp['x_prompt'].astype(f32, copy=False)
    xs = inp['x_sample'].astype(f32, copy=False)
    pvec = np.zeros((128, NPV), f32)

    def put(name, arr):
        a = arr.reshape(128, -1)
        pvec[:, PVO[name]:PVO[name] + a.shape[1]] = a
    put('nmp', _fm(inp['norm_mix_pre']))
    put('nmq', _fm(inp['norm_mix_post']))
    put('nfp', _fm(inp['norm_ffn_pre']))
    put('nfq', _fm(inp['norm_ffn_post']))
    put('lcw', _fm(inp['lru_conv_w']))
    put('lcb', _fm(inp['lru_conv_b']))
    put('lbr', _fm(inp['lru_b_r'].reshape(NA, 1024)))
    put('lbi', _fm(inp['lru_b_i'].reshape(NA, 1024)))
    put('llam', _fm(inp['lru_lambda']))
    put('scw', _fm(inp['ssd_conv_w']))
    put('scb', _fm(inp['ssd_conv_b']))
    put('sgn', _fm(inp['ssd_norm']))
    put('sdx', _fm(np.repeat(inp['ssd_d'], 64, axis=-1)))
    prow = np.zeros((128, NPR), f32)
    prow[:, PRO['dtb']:PRO['dtb'] + NB * 32] = inp['ssd_dt_bias'].reshape(1, -1)
    prow[:, PRO['alog']:PRO['alog'] + NB * 32] = inp['ssd_a_log'].reshape(1, -1)
    prow[:, PRO['even']:PRO['even'] + 32] = (np.arange(32) % 2 == 0).astype(f32)[None]
    prow[:, PRO['odd']:PRO['odd'] + 32] = (np.arange(32) % 2 == 1).astype(f32)[None]
    cons = _consts()
    shared = {
        'pvec': pvec, 'prow': prow, 'cons': cons,
        'lru_w_in': inp['lru_w_in'], 'lru_w_r': inp['lru_w_r'].reshape(NA, 1024, 256),
        'lru_w_i': inp['lru_w_i'].reshape(NA, 1024, 256), 'lru_w_out': inp['lru_w_out'],
        'ssd_w_in': inp['ssd_w_in'], 'ssd_w_out': inp['ssd_w_out'],
        'ffn_w_gate': inp['ffn_w_gate'], 'ffn_w_up': inp['ffn_w_up'], 'ffn_w_down': inp['ffn_w_down'],
    }
    in_maps = []
    for c in range(8):
        b, s = c // 4, c % 4
        xt = np.concatenate([xp[b, s * NPT:(s + 1) * NPT, :], xs[4 * c:4 * c + 4].reshape(64, DM)], axis=0)
        xin = np.ascontiguousarray(xt.T.reshape(8, 128, NTOK).transpose(1, 0, 2))
        mvec = np.zeros((128, 8), f32)
        for r in range(4):
            mvec[:, r] = 1.0 if r < s else 0.0
            mvec[:, 4 + r] = 1.0 if r == s - 1 else 0.0
        sl = slice(4 * c, 4 * c + 4)
        st_lc = np.ascontiguousarray(_fm(inp['state_lru_conv'][:, sl]).transpose(0, 1, 4, 2, 3))
        st_lh = np.ascontiguousarray(_fm(inp['state_lru_h'][:, sl]).transpose(0, 1, 3, 2))
        st_sc = np.ascontiguousarray(_fm(inp['state_ssd_conv'][:, sl]).transpose(0, 1, 4, 2, 3))
        st_ss = np.ascontiguousarray(inp['state_ssd'][:, sl].reshape(NB, 4, 2048, 128).transpose(0, 1, 3, 2))
        m = dict(shared)
        m.update({'xin': xin, 'mvec': mvec, 'st_lc': st_lc, 'st_lh': st_lh, 'st_sc': st_sc, 'st_ss': st_ss})
        in_maps.append(m)
    if RUNNER is not None:
        R = RUNNER(nc, in_maps)
    else:
        R = run_bass_kernel_spmd(nc, in_maps, core_ids=list(range(8))).results
    y_prompt = np.zeros((2, 8192, DM), f32)
    y_sample = np.zeros((32, 16, DM), f32)
    p_lc = np.zeros((NA, 2, 3, 1024), f32)
    p_lh = np.zeros((NA, 2, 1024), f32)
    p_sc = np.zeros((NB, 2, 3, 3072), f32)
    p_ss = np.zeros((NB, 2, 32, 64, 128), f32)
    s_lc = np.zeros((NA, 32, 3, 1024), f32)
    s_lh = np.zeros((NA, 32, 1024), f32)
    s_sc = np.zeros((NB, 32, 3, 3072), f32)
    s_ss = np.zeros((NB, 32, 32, 64, 128), f32)

    def unfm(a):
        a = np.moveaxis(a, 0, -1)
        return a.reshape(a.shape[:-2] + (a.shape[-2] * 128,))
    for c in range(8):
        b, s = c // 4, c % 4
        y = np.asarray(R[c]['y'])
        yt = y.transpose(2, 1, 0).reshape(NTOK, DM)
        y_prompt[b, s * NPT:(s + 1) * NPT] = yt[:NPT]
        y_sample[4 * c:4 * c + 4] = yt[NPT:].reshape(4, 16, DM)
        lc = unfm(np.asarray(R[c]['o_lc']).transpose(0, 1, 3, 4, 2))
        lh = unfm(np.asarray(R[c]['o_lh']).transpose(0, 1, 3, 2))
        sc = unfm(np.asarray(R[c]['o_sc']).transpose(0, 1, 3, 4, 2))
        ss = np.asarray(R[c]['o_ss']).transpose(0, 1, 3, 2).reshape(NB, 5, 32, 64, 128)
        s_lc[:, 4 * c:4 * c + 4] = lc[:, 1:5]
        s_lh[:, 4 * c:4 * c + 4] = lh[:, 1:5]
        s_sc[:, 4 * c:4 * c + 4] = sc[:, 1:5]
        s_ss[:, 4 * c:4 * c + 4] = ss[:, 1:5]
        if s == 3:
            p_lc[:, b] = lc[:, 0]
            p_lh[:, b] = lh[:, 0]
            p_sc[:, b] = sc[:, 0]
            p_ss[:, b] = ss[:, 0]
    return (y_prompt, y_sample, p_lc, p_lh, p_sc, p_ss, s_lc, s_lh, s_sc, s_ss)
```

```python
import contextlib
import numpy as np
import concourse.bass as bass
import concourse.mybir as mybir
from concourse.bass_utils import run_bass_kernel_spmd

F32 = mybir.dt.float32
BF16 = mybir.dt.bfloat16
AF = mybir.ActivationFunctionType
ALU = mybir.AluOpType

ENGS = ['tensor', 'vector', 'scalar', 'gpsimd', 'sync']
NDMASEM = 12

DEPTH = 4
NA, NB = 2, 2
DM = 1024
NPT = 2048
NTOK = 2112
NT = 512
DFF = 2816
EPS = 1e-6
RG = [[0, 1, 2, 3], [4, 5, 6, 7]]
SLOT = 5632


class Op:
    __slots__ = ('eng', 'fn', 'deps', 'idx', 'inc', 'semval', 'dma', 'dsem', 'dval', 'dprev')


class Sched:
    def __init__(self):
        self.ops = []
        self.last_w = {}
        self.readers = {}
        self.dma_count = {e: 0 for e in ENGS}
        self.recording = False

    def add(self, eng, fn, reads=(), writes=(), dma=False):
        if self.recording:
            return -1
        op = Op()
        op.eng = eng
        op.fn = fn
        op.idx = len(self.ops)
        op.dma = dma
        op.inc = False
        op.semval = 0
        deps = set()
        for r in reads:
            w = self.last_w.get(r)
            if w is not None:
                deps.add(w)
        for k in writes:
            w = self.last_w.get(k)
            if w is not None:
                deps.add(w)
            for r in self.readers.get(k, ()):
                deps.add(r)
        best = {}
        keep = set()
        for d in deps:
            p = self.ops[d]
            if p.dma:
                keep.add(d)
            elif best.get(p.eng, -1) < d:
                best[p.eng] = d
        keep.update(best.values())
        op.deps = keep
        for r in reads:
            self.readers.setdefault(r, []).append(op.idx)
        for k in writes:
            self.last_w[k] = op.idx
            self.readers[k] = []
        if dma:
            n = self.dma_count[eng]
            self.dma_count[eng] = n + 1
            op.dsem = n % NDMASEM
            op.dval = 16 * (n // NDMASEM + 1)
            op.dprev = op.dval - 16
        self.ops.append(op)
        return op.idx

    def emit(self, nc, stack):
        ops = self.ops
        for op in ops:
            for d in op.deps:
                p = ops[d]
                if p.dma:
                    continue
                if p.eng == 'tensor' and op.eng == 'tensor' and not op.dma:
                    continue
                p.inc = True
        cnt = {e: 0 for e in ENGS}
        for op in ops:
            if op.inc and not op.dma:
                cnt[op.eng] += 1
                op.semval = cnt[op.eng]
        esem = {e: stack.enter_context(nc.semaphore('es_' + e)) for e in ENGS}
        dsem = {}
        for e in ENGS:
            if self.dma_count[e]:
                dsem[e] = [stack.enter_context(nc.semaphore('ds_%s_%d' % (e, i)))
                           for i in range(min(NDMASEM, self.dma_count[e]))]
        per = {e: [] for e in ENGS}
        for op in ops:
            per[op.eng].append(op)
        block = stack.enter_context(nc.Block())
        self.counts = dict(cnt)

        def run(engname, eng):
            waited = {}

            def wait(sem, key, val):
                if val <= 0 or waited.get(key, 0) >= val:
                    return
                waited[key] = val
                eng.wait_ge(sem, val)

            for op in per[engname]:
                for d in sorted(op.deps):
                    p = ops[d]
                    if p.dma:
                        wait(dsem[p.eng][p.dsem], ('d', p.eng, p.dsem), p.dval)
                    else:
                        if p.eng == 'tensor' and engname == 'tensor' and not op.dma:
                            continue
                        wait(esem[p.eng], ('e', p.eng), p.semval)
                if op.dma:
                    wait(dsem[engname][op.dsem], ('d', engname, op.dsem), op.dprev)
                ins = op.fn(eng)
                if op.dma:
                    ins.then_inc(dsem[engname][op.dsem], 16)
                elif op.inc:
                    ins.then_inc(esem[engname], 1)
            if engname == 'sync':
                for e in ENGS:
                    n = self.dma_count[e]
                    for i in range(min(NDMASEM, n)):
                        tot = 16 * ((n - 1 - i) // NDMASEM + 1)
                        wait(dsem[e][i], ('d', e, i), tot)
                for e in ENGS:
                    if e != 'sync' and cnt[e]:
                        wait(esem[e], ('e', e), cnt[e])

        @block.tensor
        def _(e):
            run('tensor', e)

        @block.vector
        def _(e):
            run('vector', e)

        @block.scalar
        def _(e):
            run('scalar', e)

        @block.gpsimd
        def _(e):
            run('gpsimd', e)

        @block.sync
        def _(e):
            run('sync', e)


def _pv_layout():
    items = [('nmp', 32), ('nmq', 32), ('nfp', 32), ('nfq', 32),
             ('lcw', NA * 4 * 8), ('lcb', NA * 8), ('lbr', NA * 8), ('lbi', NA * 8), ('llam', NA * 8),
             ('scw', NB * 4 * 24), ('scb', NB * 24), ('sgn', NB * 16), ('sdx', NB * 16)]
    off = {}
    o = 0
    for k, n in items:
        off[k] = o
        o += n
    return off, o


PVO, NPV = _pv_layout()
PRO = {'dtb': 0, 'alog': NB * 32, 'even': 2 * NB * 32, 'odd': 2 * NB * 32 + 32}
NPR = 2 * NB * 32 + 64
CO = {'ident': 0, 'U64': 128, 'T64': 192, 'CM64': 256, 'U16': 320, 'T16': 384, 'CM16': 448,
      'MREP': 512, 'SUBM': 512 + 5 * 128}
NCO = 512 + 5 * 128 + 4


def build_program(depth=DEPTH, kinds=None):
    kinds = kinds or ['lru' if l % 2 == 0 else 'ssd' for l in range(depth)]
    jidx = [sum(1 for q in kinds[:l] if q == kinds[l]) for l in range(depth)]
    nc = bass.Bass("TRN2", target_bir_lowering=False)

    def din(name, shape, dt=F32):
        return nc.dram_tensor(name, shape, dt, kind="ExternalInput").ap()

    def dout(name, shape, dt=F32):
        return nc.dram_tensor(name, shape, dt, kind="ExternalOutput").ap()

    def dscr(name, shape, dt=F32):
        return nc.dram_tensor(name, shape, dt).ap()

    xin = din("xin", [128, 8, NTOK])
    pvec_d = din("pvec", [128, NPV])
    prow_d = din("prow", [128, NPR])
    cons_d = din("cons", [128, NCO])
    mvec_d = din("mvec", [128, 8])
    st_lc = din("st_lc", [128, NA, 8, 4, 3])
    st_lh = din("st_lh", [128, NA, 8, 4])
    st_sc = din("st_sc", [128, NB, 24, 4, 3])
    st_ss = din("st_ss", [NB, 4, 128, 2048])
    W = {
        'lru_w_in': din("lru_w_in", [NA, DM, 2048]),
        'lru_w_r': din("lru_w_r", [NA, 1024, 256]),
        'lru_w_i': din("lru_w_i", [NA, 1024, 256]),
        'lru_w_out': din("lru_w_out", [NA, DM, DM]),
        'ssd_w_in': din("ssd_w_in", [NB, DM, 5152]),
        'ssd_w_out': din("ssd_w_out", [NB, 2048, DM]),
        'ffn_w_gate': din("ffn_w_gate", [DEPTH, DM, DFF]),
        'ffn_w_up': din("ffn_w_up", [DEPTH, DM, DFF]),
        'ffn_w_down': din("ffn_w_down", [DEPTH, DFF, DM]),
    }
    yout = dout("y", [128, 8, NTOK])
    o_lc = dout("o_lc", [128, NA, 8, 5, 3])
    o_lh = dout("o_lh", [128, NA, 8, 5])
    o_sc = dout("o_sc", [128, NB, 24, 5, 3])
    o_ss = dout("o_ss", [NB, 5, 128, 2048])

    WB = {k: dscr(k + "_bf", list(v.shape), BF16) for k, v in W.items()}
    wdt_bf = dscr("wdt_bf", [NB, 128, 256], BF16)
    xsc = dscr("xsc", [128, 8, NTOK])
    ag0_in = dscr("ag0_in", [128, 24])
    ag0_out = dscr("ag0_out", [4 * 128, 24])
    agl_in = dscr("agl_in", [128, 16])
    agl_out = dscr("agl_out", [4 * 128, 16])
    ags_in = dscr("ags_in", [128, 2048])
    ags_out = dscr("ags_out", [4 * 128, 2048])
    agd_in = dscr("agd_in", [128, 32])
    agd_out = dscr("agd_out", [4 * 128, 32])

    S = Sched()
    st = contextlib.ExitStack()
    with st:
        def sb(name, shape, dt=F32):
            return st.enter_context(nc.sbuf_tensor(name, shape, dt))

        def psum(name, shape, dt=F32):
            return st.enter_context(nc.psum_tensor(name, shape, dt))

        XT = [sb("XT%d" % i, [128, 8, NT]) for i in range(2)]
        SLOTS = [sb("SL%d" % i, [128, SLOT], BF16) for i in range(3)]
        WRI = sb("WRI", [128, 4096], BF16)
        HN = sb("HN", [128, 8, NT], BF16)
        HNH = sb("HNH", [128, 8, 4], BF16)
        ACTB = sb("ACTB", [128, 22, NT], BF16)
        MIXS = sb("MIXS", [128, 8, NT])
        SQ = sb("SQ", [128, 8, NT], BF16)
        RS0 = sb("RS0", [128, NT])
        RSTD = sb("RSTD", [128, NT])
        PV = sb("PV", [128, NPV])
        PR = sb("PR", [128, NPR])
        CN = sb("CN", [128, NCO])
        MV = sb("MV", [128, 8])
        OMV = sb("OMV", [128, 8])
        IDB = sb("IDB", [128, 128], BF16)
        ONESB = sb("ONESB", [128, 128], BF16)
        KC = sb("KC", [128, 4])
        ABC = sb("ABC", [128, NB * 32])
        C1 = sb("C1", [128, NA * 8])
        CARRY_L = sb("CARRY_L", [128, 8, 5, 3])
        TAIL_L = sb("TAIL_L", [128, 8, 3])
        HST = sb("HST", [128, 8, 5])
        APROD = sb("APROD", [128, 8])
        HALO_ALL = sb("HALO_ALL", [128, 4, 24])
        XHALO = sb("XHALO", [128, 8, 3])
        AGL = sb("AGL", [128, 4, 16])
        XBC = sb("XBC", [128, 6, NT], BF16)
        CBS = [sb("CBS%d" % i, [128, 520]) for i in range(2)]
        XCS = [sb("XCS%d" % i, [128, NT]) for i in range(2)]
        YY = sb("YY", [128, 4, NT])
        RHSD = sb("RHSD", [64, 512])
        LT = sb("LT", [64, 512])
        MT = sb("MT", [64, 512], BF16)
        XTOK = sb("XTOK", [64, 512], BF16)
        BTOK = sb("BTOK", [64, 128], BF16)
        BTM = sb("BTM", [64, 4, 128], BF16)
        XDTE = sb("XDTE", [64, 512], BF16)
        XDTO = sb("XDTO", [64, 512], BF16)
        XDTD = sb("XDTD", [64, 512], BF16)
        DAX = sb("DAX", [64, 512])
        CBM = sb("CBM", [64, 64])
        EACS = sb("EACS", [128, 256])
        YTMP = sb("YTMP", [128, 256])
        DT = sb("DT", [64, 8, 32])
        DTE = sb("DTE", [64, 8, 32])
        DTO = sb("DTO", [64, 8, 32])
        DA = sb("DA", [64, 8, 32])
        DEC = sb("DEC", [64, 8, 32])
        DTDEC = sb("DTDEC", [64, 8, 32])
        BD = sb("BD", [128, 8, 32])
        BDTOT = sb("BDTOT", [128, 32])
        SP = sb("SP", [128, 4, 512])
        SBF = sb("SBF", [128, 4, 512], BF16)
        CARRY_S = sb("CARRY_S", [128, 24, 5, 3])
        STG_S = sb("STG_S", [128, 24, 4, 3])
        TAIL_S = sb("TAIL_S", [128, 24, 3])
        AGT = sb("AGT", [128, 2080])
        COEF = sb("COEF", [128, 32])

        TMPF = XCS
        ZG = SQ[:, 4:8, :]
        XC = YY[:, 0:2, :]
        GG = YY[:, 2:4, :]
        AA = SP[:, 0:2, :]
        UU = SP[:, 2:4, :]
        SS_ = AGT[:, 0:1024].rearrange("p (m t) -> p m t", m=2)
        HH = AGT[:, 1040:2064].rearrange("p (m t) -> p m t", m=2)
        XCB = SBF[:, 0:2, :]

        MXA = MIXS[:].rearrange("p c t -> p (c t)")
        LXC = [XC, MXA[:, 0:1024].rearrange("p (m t) -> p m t", m=2)]
        LGG = [GG, MXA[:, 1024:2048].rearrange("p (m t) -> p m t", m=2)]
        LCB = [[CBS[0][:, :], CBS[1][:, :]], [MXA[:, 2048:2568], MXA[:, 2568:3088]]]
        LXCB = [XCB, MXA[:, 3200:3712].bitcast(BF16).rearrange("p (m t) -> p m t", m=2)]
        MXF = MIXS[:, 0:5, :].rearrange("p c t -> p (c t)")
        MXB = MIXS[:, 5:8, :].rearrange("p c t -> p (c t)").bitcast(BF16)
        TS = [
            {'RHSD': RHSD[:, :], 'LT': LT[:, :], 'DAX': DAX[:, :], 'CBM': CBM[:, :], 'EACS': EACS[:, :],
             'YTMP': YTMP[:, :], 'MT': MT[:, :], 'XTOK': XTOK[:, :], 'BTOK': BTOK[:, :], 'XDTE': XDTE[:, :],
             'XDTO': XDTO[:, :], 'XDTD': XDTD[:, :]},
            {'RHSD': MXF[0:64, 0:512], 'LT': MXF[0:64, 512:1024], 'DAX': MXF[0:64, 1024:1536],
             'CBM': MXF[0:64, 1536:1600], 'EACS': MXF[:, 1664:1920], 'YTMP': MXF[:, 1920:2176],
             'MT': MXB[0:64, 0:512], 'XTOK': MXB[0:64, 512:1024], 'BTOK': MXB[0:64, 1024:1152],
             'XDTE': MXB[0:64, 1536:2048], 'XDTO': MXB[0:64, 2048:2560], 'XDTD': MXB[0:64, 2560:3072]},
        ]

        PBANK = [psum("PB%d" % i, [128, 512]) for i in range(6)]
        PTB = [psum("PT%d" % i, [128, 1024], BF16) for i in range(2)]
        pstate = {'pb': 0, 'pt': 0, 'slot': 0, 'xt': 0, 'cast': 0}

        def pb():
            i = pstate['pb']
            pstate['pb'] = (i + 1) % len(PBANK)
            return PBANK[i], ('pb', i)

        def ptb():
            i = pstate['pt']
            pstate['pt'] = (i + 1) % len(PTB)
            return PTB[i], ('pt', i)

        def V(fn, r, w):
            S.add('vector', fn, r, w)

        def A(fn, r, w):
            S.add('scalar', fn, r, w)

        def P(fn, r, w):
            S.add('tensor', fn, r, w)

        def DM_(fn, r, w, q='sync'):
            S.add(q, fn, r, w, dma=True)

        def vcopy(o, i, r, w):
            V(lambda e, o=o, i=i: e.tensor_copy(out=o, in_=i), r, w)

        def acopy(o, i, r, w):
            A(lambda e, o=o, i=i: e.copy(out=o, in_=i), r, w)

        def act(o, i, f, r, w, bias=None, scale=None):
            kw = {}
            if bias is not None:
                kw['bias'] = bias
            if scale is not None:
                kw['scale'] = scale
            A(lambda e, o=o, i=i, f=f, kw=kw: e.activation(out=o, in_=i, func=f, **kw), r, w)

        def tt(o, a, b, op, r, w):
            V(lambda e, o=o, a=a, b=b, op=op: e.tensor_tensor(out=o, in0=a, in1=b, op=op), r, w)

        def stt(o, a, s, b, op0, op1, r, w):
            V(lambda e, o=o, a=a, s=s, b=b, op0=op0, op1=op1:
              e.scalar_tensor_tensor(out=o, in0=a, scalar=s, in1=b, op0=op0, op1=op1), r, w)

        def ts(o, a, s1, s2, op0, op1, r, w):
            V(lambda e, o=o, a=a, s1=s1, s2=s2, op0=op0, op1=op1:
              e.tensor_scalar(out=o, in0=a, scalar1=s1, scalar2=s2, op0=op0, op1=op1), r, w)

        def mm(o, l, rr, start, stop, r, w):
            P(lambda e, o=o, l=l, rr=rr, start=start, stop=stop:
              e.matmul(o, l, rr, start=start, stop=stop), r, w)

        def dma(o, i, r, w, q='sync'):
            DM_(lambda e, o=o, i=i: e.dma_start(out=o, in_=i), r, w, q)

        def pv(name, idx):
            c = PVO[name] + idx
            return PV[:, c:c + 1]

        cast_q = {}

        def queue_cast(tag, name, lyr):
            rows = W[name].shape[1]
            for r0 in range(0, rows, 256):
                r1 = min(rows, r0 + 256)
                cast_q.setdefault(tag, []).append((name, lyr, r0, r1))

        for layer in range(depth):
            j = jidx[layer]
            if kinds[layer] == 'lru':
                for nm in ('lru_w_in', 'lru_w_r', 'lru_w_i', 'lru_w_out'):
                    queue_cast(('mix', layer), nm, j)
            else:
                for nm in ('ssd_w_in', 'ssd_w_out'):
                    queue_cast(('mix', layer), nm, j)
            for nm in ('ffn_w_gate', 'ffn_w_up', 'ffn_w_down'):
                queue_cast(('ffn', layer), nm, layer)

        cast_order = []
        for layer in range(depth):
            if kinds[layer] == 'ssd':
                cast_order.append(('__wdt__', jidx[layer], 0, 0))
            cast_order += cast_q.pop(('mix', layer), [])
            cast_order += cast_q.pop(('ffn', layer), [])

        def pump_casts(n, depth_=1):
            if S.recording:
                return
            for _ in range(n):
                if not cast_order:
                    return
                name, lyr, r0, r1 = cast_order.pop(0)
                if name == '__wdt__':
                    ci_ = pstate['cast']
                    pstate['cast'] = ci_ + 1
                    dma(wdt_bf[lyr].rearrange("p (k m) -> p k m", k=8),
                        W['ssd_w_in'][lyr, :, 5120:5152].rearrange("(k p) m -> p k m", p=128),
                        [('castchain', ci_ - 2)], [('wb', 'wdt', lyr), ('castchain', ci_)], q='gpsimd')
                    continue
                subs = list(range(r0, r1, 64))
                prev = []
                for si, a in enumerate(subs):
                    b_ = min(r1, a + 64)
                    ci_ = pstate['cast']
                    pstate['cast'] = ci_ + 1
                    last = si == len(subs) - 1
                    wkey = ('wb', name, lyr, r0 // 256) if last else ('wbs', name, lyr, a)
                    dma(WB[name][lyr, a:b_, :], W[name][lyr, a:b_, :], [('castchain', ci_ - depth_)] + (prev if last else []),
                        [wkey, ('castchain', ci_)], q='gpsimd')
                    prev.append(wkey)

        def wkeys(name, lyr, r0, r1):
            return [('wb', name, lyr, b) for b in range(r0 // 256, (r1 + 255) // 256)]

        def load_unit(pieces, buf=None, key=None):
            if buf is None:
                i = pstate['slot']
                pstate['slot'] = (i + 1) % len(SLOTS)
                buf = SLOTS[i]
                key = ('slot', i)
            offs = []
            off = 0
            for (name, lyr, r0, nr, c0, ncol) in pieces:
                nk = nr // 128
                dst = buf[:, off:off + nk * ncol].rearrange("p (k m) -> p k m", k=nk)
                src = WB[name][lyr, r0:r0 + nr, c0:c0 + ncol].rearrange("(k p) m -> p k m", p=128)
                for kk_ in wkeys(name, lyr, r0, r0 + nr):
                    assert S.recording or kk_ in S.last_w, kk_
                dma(dst, src, wkeys(name, lyr, r0, r0 + nr), [key])
                offs.append((off, ncol))
                off += nk * ncol
            assert off <= buf.shape[1]

            def w(pi, k, m0, m1):
                o, ncol = offs[pi]
                return buf[:, o + k * ncol + m0:o + k * ncol + m1]
            return w, key

        dma(PV[:], pvec_d, [], ['PV0'])
        gb = PV[:, PVO['lbr']:PVO['lbr'] + 2 * NA * 8]
        ts(gb, gb, 0.5, None, ALU.mult, ALU.bypass, ['PV0'], ['PV'])
        dma(PR[:], prow_d, [], ['PR'])
        dma(CN[:], cons_d, [], ['CN'])
        dma(MV[:], mvec_d, [], ['MV'])
        vcopy(IDB[:], CN[:, 0:128], ['CN'], ['IDB'])
        V(lambda e: e.memset(ONESB[:], 1.0), [], ['ONESB'])
        V(lambda e: e.memset(KC[:, 0:1], EPS), [], ['KC0'])
        V(lambda e: e.memset(KC[:, 1:2], 1.0), [], ['KC'])
        ts(OMV[:], MV[:], -1.0, 1.0, ALU.mult, ALU.add, ['MV'], ['OMV'])
        act(ABC[:], PR[:, PRO['alog']:PRO['alog'] + NB * 32], AF.Exp, ['PR'], ['ABC0'])
        ts(ABC[:], ABC[:], -1.0, None, ALU.mult, ALU.bypass, ['ABC0'], ['ABC'])
        lam = PV[:, PVO['llam']:PVO['llam'] + NA * 8]
        act(C1[:], lam, AF.Exp, ['PV'], ['C1a'], scale=-1.0)
        act(C1[:], C1[:], AF.Ln, ['C1a', 'KC'], ['C1b'], bias=KC[:, 1:2])
        ts(C1[:], C1[:], -4.0, None, ALU.mult, ALU.bypass, ['C1b'], ['C1'])

        V(lambda e: e.memset(KC[:, 2:3], -0.6931471805599453), [], ['KC2'])

        def cm(name, n=64):
            return CN[0:64, CO[name]:CO[name] + n]

        tiles = [(t * NT, NT, 1, NT, 0) for t in range(NPT // NT)] + [(NPT, 64, 4, 16, 1)]
        ptiles = tiles[:-1]

        XSEQ = []
        xst = {'i': 0, 'pref': set()}

        def _issue_x(i):
            src, tile = XSEQ[i]
            col0, N = tile[0], tile[1]
            b = i % 2
            dma(XT[b][:, :, :N], src[:, :, col0:col0 + N], [('xd', src.name, col0)], [('XT', b)])

        def load_x(src, tile):
            if S.recording:
                XSEQ.append((src, tile))
                return XT[0], ('XT', 0)
            i = xst['i']
            assert XSEQ[i][0].name == src.name and XSEQ[i][1] == tile
            if i not in xst['pref']:
                _issue_x(i)
            return XT[i % 2], ('XT', i % 2)

        def prefetch_x():
            if S.recording:
                return
            i = xst['i'] + 1
            if i < len(XSEQ) and i not in xst['pref']:
                xst['pref'].add(i)
                _issue_x(i)

        def end_x():
            if not S.recording:
                xst['i'] += 1

        def store_x(dst, tile, xt, key):
            col0, N = tile[0], tile[1]
            dma(dst[:, :, col0:col0 + N], xt[:, :, :N], [key], [('xd', dst.name, col0)])

        def rstd_from_sq(nch, N, nfeat, sqkeys):
            bank, bk = pb()
            for c in range(nch):
                mm(bank[:, :N], ONESB[:, :], SQ[:, c, :N], c == 0, c == nch - 1, [sqkeys[c], 'ONESB'], [bk])
            act(RS0[:, :N], bank[:, :N], AF.Ln, [bk, 'KC0'], ['RS0'], bias=KC[:, 0:1], scale=1.0 / nfeat)
            act(RSTD[:, :N], RS0[:, :N], AF.Exp, ['RS0'], ['RSTD'], scale=-0.5)

        def prenorm(xap, xkey, N, gname, layer, out=None, okey='HN'):
            out = HN if out is None else out
            A(lambda e, N=N: e.activation(out=SQ[:, :, :N], in_=xap, func=AF.Square), [xkey],
              [('SQ', c) for c in range(8)])
            rstd_from_sq(8, N, DM, [('SQ', c) for c in range(8)])
            for c in range(8):
                stt(out[:, c, :N], xap[:, c, :], pv(gname, layer * 8 + c), RSTD[:, :N], ALU.mult, ALU.mult,
                    [xkey, 'RSTD', 'PV'], [(okey, c)])

        def evac_post(bank, bk, mg, N):
            acopy(MIXS[:, mg, :N], bank[:, :N], [bk], [('MIXS', mg)])
            act(SQ[:, mg, :N], bank[:, :N], AF.Square, [bk], [('SQ', mg)])

        def postnorm_residual(xt, xkey, N, gname, layer):
            rstd_from_sq(8, N, DM, [('SQ', c) for c in range(8)])
            for c in range(8):
                stt(MIXS[:, c, :N], MIXS[:, c, :N], pv(gname, layer * 8 + c), RSTD[:, :N], ALU.mult, ALU.mult,
                    [('MIXS', c), 'RSTD', 'PV'], [('MIXS', c)])
                tt(xt[:, c, :N], xt[:, c, :N], MIXS[:, c, :N], ALU.add, [('MIXS', c), xkey], [xkey])

        def ffn_tile(layer, tile, src, dst, halo_next=None):
            N = tile[1]
            xt, xkey = load_x(src, tile)
            prenorm(xt[:, :, :N], xkey, N, 'nfp', layer)
            hk = [('HN', k) for k in range(8)]
            for u in range(DFF // 256):
                if u == 4:
                    prefetch_x()
                w, wk = load_unit([('ffn_w_gate', layer, 0, DM, u * 256, 256), ('ffn_w_up', layer, 0, DM, u * 256, 256)])
                for m in range(2):
                    bg, bgk = pb()
                    for k in range(8):
                        mm(bg[:, :N], w(0, k, m * 128, m * 128 + 128), HN[:, k, :N], k == 0, k == 7, [wk, hk[k]], [bgk])
                    bu, buk = pb()
                    for k in range(8):
                        mm(bu[:, :N], w(1, k, m * 128, m * 128 + 128), HN[:, k, :N], k == 0, k == 7, [wk, hk[k]], [buk])
                    tf = TMPF[m]
                    act(tf[:, :N], bg[:, :N], AF.Silu, [bgk], [('XCS', m)])
                    tt(ACTB[:, 2 * u + m, :N], tf[:, :N], bu[:, :N], ALU.mult, [('XCS', m), buk], [('ACTB', 2 * u + m)])
            for u in range(4):
                w, wk = load_unit([('ffn_w_down', layer, 0, DFF, u * 256, 256)])
                for m in range(2):
                    bank, bk = pb()
                    for k in range(22):
                        mm(bank[:, :N], w(0, k, m * 128, m * 128 + 128), ACTB[:, k, :N], k == 0, k == 21,
                           [wk, ('ACTB', k)], [bk])
                    evac_post(bank, bk, 2 * u + m, N)
            postnorm_residual(xt, xkey, N, 'nfq', layer)
            store_x(dst, tile, xt, xkey)
            if halo_next is not None:
                halo_exchange(None, halo_next, xt, xkey, N)
            end_x()

        def conv_chunk(bank, bk, cbuf, cbk, xc, xck, carry, ck, N, nseq, L, wname, bname, widx, bidx):
            cb3 = cbuf[:, 0:nseq * (3 + L)].rearrange("p (s t) -> p s t", s=nseq)
            vcopy(cb3[:, :, 0:3], carry, [ck], [cbk])
            acopy(cb3[:, :, 3:3 + L], bank[:, :N].rearrange("p (s t) -> p s t", s=nseq), [bk], [cbk])
            vcopy(carry, cb3[:, :, L:L + 3], [cbk], [ck])
            xc3 = xc.rearrange("p (s t) -> p s t", s=nseq)
            act(xc3, bank[:, :N].rearrange("p (s t) -> p s t", s=nseq), AF.Identity, [bk, 'PV'], [xck],
                bias=pv(bname, bidx), scale=pv(wname, widx(3)))
            for k in range(0, 3):
                stt(xc3, cb3[:, :, k:k + L], pv(wname, widx(k)), xc3, ALU.mult, ALU.add, [cbk, xck, 'PV'], [xck])

        def halo_exchange(src, layer, xt=None, xkey=None, N=None):
            if xt is None:
                dma(ag0_in.rearrange("p (c t) -> p c t", c=8), src[:, :, NPT - 3:NPT], [('xd', src.name, NPT - NT)],
                    ['ag0_in'])
            else:
                vcopy(HALO_ALL[:, 0, :].rearrange("p (c t) -> p c t", c=8), xt[:, :, N - 3:N], [xkey], ['HALO_STG'])
                dma(ag0_in, HALO_ALL[:, 0, :], ['HALO_STG'], ['ag0_in'])
            S.add('gpsimd', lambda e: e.collective_compute("AllGather", ALU.bypass, replica_groups=RG,
                                                           ins=[ag0_in.opt()], outs=[ag0_out.opt()]),
                  ['ag0_in'], ['ag0_out'])
            if layer == 1:
                pump_casts(19)
            dma(HALO_ALL[:], ag0_out.rearrange("(r p) f -> p r f", p=128), ['ag0_out', 'HALO_STG'], ['HALO_ALL'])
            xh = XHALO[:].rearrange("p c t -> p (c t)")
            ts(xh, HALO_ALL[:, 0, :], MV[:, 4:5], None, ALU.mult, ALU.bypass, ['HALO_ALL', 'MV'], ['XHALO'])
            for r in range(1, 4):
                stt(xh, HALO_ALL[:, r, :], MV[:, 4 + r:5 + r], xh, ALU.mult, ALU.add, ['HALO_ALL', 'MV', 'XHALO'],
                    ['XHALO'])
            prenorm(XHALO[:, :, :], 'XHALO', 3, 'nmp', layer, out=HNH, okey='HNH')

        def lru_tile(layer, tile, full, src, dst, first):
            j = jidx[layer]
            col0, N, nseq, L, seq0 = tile
            xt, xkey = load_x(src, tile)
            prenorm(xt[:, :, :N], xkey, N, 'nmp', layer)
            hk = [('HN', k) for k in range(8)]
            wri, wrik = load_unit([('lru_w_r', j, b * 256, 256, 0, 256) for b in range(4)] +
                                  [('lru_w_i', j, b * 256, 256, 0, 256) for b in range(4)], buf=WRI, key='WRI')
            def lru_front(b):
                sx = b % 2
                pieces = [('lru_w_in', j, 0, DM, 1024 + b * 256, 256)]
                if full:
                    pieces.append(('lru_w_in', j, 0, DM, b * 256, 256))
                w, wk = load_unit(pieces)
                if first:
                    for m in range(2):
                        c = 2 * b + m
                        bank, bk = pb()
                        for k in range(8):
                            mm(bank[:, :3], w(0, k, m * 128, m * 128 + 128), HNH[:, k, :3], k == 0, k == 7,
                               [wk, ('HNH', k)], [bk])
                        acopy(TAIL_L[:, c, :], bank[:, :3], [bk], [('TAIL_L', c)])
                        vcopy(CARRY_L[:, c, 0, :], TAIL_L[:, c, :], [('TAIL_L', c)], [('CARRY_L', c)])
                for m in range(2):
                    c = 2 * b + m
                    bank, bk = pb()
                    for k in range(8):
                        mm(bank[:, :N], w(0, k, m * 128, m * 128 + 128), HN[:, k, :N], k == 0, k == 7, [wk, hk[k]], [bk])
                    conv_chunk(bank, bk, LCB[sx][m], ('CBS', sx, m), LXC[sx][:, m, :N], ('XC', sx, m),
                               CARRY_L[:, c, seq0:seq0 + nseq, :], ('CARRY_L', c), N, nseq, L,
                               'lcw', 'lcb', lambda k, c=c: (j * 4 + k) * 8 + c, j * 8 + c)
                if full:
                    for m in range(2):
                        bank, bk = pb()
                        for k in range(8):
                            mm(bank[:, :N], w(1, k, m * 128, m * 128 + 128), HN[:, k, :N], k == 0, k == 7, [wk, hk[k]], [bk])
                        act(LGG[sx][:, m, :N], bank[:, :N], AF.Gelu_apprx_tanh, [bk], [('GG', sx, m)])
                for m in range(2):
                    acopy(LXCB[sx][:, m, :N], LXC[sx][:, m, :N], [('XC', sx, m)], [('XCB', sx, m)])
            def lru_back(b):
                sx = b % 2
                banks = []
                for mo in range(2):
                    br, brk = pb()
                    for kk in range(2):
                        mm(br[:, :N], wri(b, kk, mo * 128, mo * 128 + 128), LXCB[sx][:, kk, :N], kk == 0, kk == 1,
                           [wrik, ('XCB', sx, kk)], [brk])
                    bi, bik = pb()
                    for kk in range(2):
                        mm(bi[:, :N], wri(4 + b, kk, mo * 128, mo * 128 + 128), LXCB[sx][:, kk, :N], kk == 0, kk == 1,
                           [wrik, ('XCB', sx, kk)], [bik])
                    banks.append((br, brk, bi, bik))
                for mo in range(2):
                    c = 2 * b + mo
                    br, brk, bi, bik = banks[mo]
                    act(AA[:, mo, :N], br[:, :N], AF.Tanh, [brk, 'PV'], [('AA', mo)],
                        bias=pv('lbr', j * 8 + c), scale=0.5)
                    act(UU[:, mo, :N], bi[:, :N], AF.Tanh, [bik, 'PV'], [('UU', mo)],
                        bias=pv('lbi', j * 8 + c), scale=0.5)
                for mo in range(2):
                    c = 2 * b + mo
                    act(AA[:, mo, :N], AA[:, mo, :N], AF.Exp, [('AA', mo), 'C1'], [('AA', mo)],
                        scale=C1[:, j * 8 + c:j * 8 + c + 1], bias=C1[:, j * 8 + c:j * 8 + c + 1])
                    tt(SS_[:, mo, :N], AA[:, mo, :N], AA[:, mo, :N], ALU.mult, [('AA', mo)], [('SSQ', mo)])
                for mo in range(2):
                    act(SS_[:, mo, :N], SS_[:, mo, :N], AF.Ln, [('SSQ', mo), 'KC'], [('SSQ', mo)], bias=KC[:, 1:2],
                        scale=-1.0)
                    act(SS_[:, mo, :N], SS_[:, mo, :N], AF.Exp, [('SSQ', mo), 'KC2'], [('SSQ', mo)], scale=0.5,
                        bias=KC[:, 2:3])
                for mo in range(2):
                    c = 2 * b + mo
                    stt(UU[:, mo, :N], UU[:, mo, :N], 1.0, LXC[sx][:, mo, :N], ALU.add, ALU.mult,
                        [('UU', mo), ('XC', sx, mo)], [('UU', mo)])
                    tt(UU[:, mo, :N], UU[:, mo, :N], SS_[:, mo, :N], ALU.mult, [('UU', mo), ('SSQ', mo)], [('UU', mo)])
                    for s_ in range(nseq):
                        sid = seq0 + s_
                        sl = slice(s_ * L, (s_ + 1) * L)
                        V(lambda e, mo=mo, sl=sl, c=c, sid=sid: e.tensor_tensor_scan(
                            out=HH[:, mo, sl], data0=AA[:, mo, sl], data1=UU[:, mo, sl],
                            initial=HST[:, c, sid:sid + 1], op0=ALU.mult, op1=ALU.add),
                          [('AA', mo), ('UU', mo), ('HST', c)], [('HH', mo)])
                        vcopy(HST[:, c, sid:sid + 1], HH[:, mo, (s_ + 1) * L - 1:(s_ + 1) * L], [('HH', mo)], [('HST', c)])
                    if not full:
                        V(lambda e, mo=mo, c=c, N=N: e.tensor_tensor_scan(
                            out=SS_[:, mo, :N], data0=AA[:, mo, :N], data1=TMPF[1][:, :N],
                            initial=APROD[:, c:c + 1], op0=ALU.mult, op1=ALU.add),
                          [('AA', mo), ('XCS', 1), ('APROD', c), ('SSQ', mo)], [('SSQ', mo)])
                        vcopy(APROD[:, c:c + 1], SS_[:, mo, N - 1:N], [('SSQ', mo)], [('APROD', c)])
                    if full:
                        tt(ACTB[:, c, :N], HH[:, mo, :N], LGG[sx][:, mo, :N], ALU.mult, [('HH', mo), ('GG', sx, mo)],
                           [('ACTB', c)])
            lru_front(0)
            for b in range(4):
                if b + 1 < 4:
                    lru_front(b + 1)
                lru_back(b)
                if b == 1:
                    prefetch_x()
            if not full:
                end_x()
                return
            for u in range(2):
                w, wk = load_unit([('lru_w_out', j, 0, DM, u * 512, 512)])
                for m in range(4):
                    bank, bk = pb()
                    for k in range(8):
                        mm(bank[:, :N], w(0, k, m * 128, m * 128 + 128), ACTB[:, k, :N], k == 0, k == 7,
                           [wk, ('ACTB', k)], [bk])
                    evac_post(bank, bk, 4 * u + m, N)
            postnorm_residual(xt, xkey, N, 'nmq', layer)
            store_x(dst, tile, xt, xkey)
            end_x()

        def lru_layer(layer, src, dst):
            j = jidx[layer]
            hk_all = [('HST', c) for c in range(8)]
            ck_all = [('CARRY_L', c) for c in range(8)]
            V(lambda e: e.memset(HST[:, :, 0:1], 0.0), [], hk_all)
            V(lambda e: e.memset(APROD[:], 1.0), [], [('APROD', c) for c in range(8)])
            V(lambda e: e.memset(TMPF[1][:, :], 0.0), [], [('XCS', 1)])
            for ti, tile in enumerate(ptiles):
                lru_tile(layer, tile, False, src, dst, ti == 0)
            vcopy(AGL[:, 0, 0:8], HST[:, :, 0], hk_all, ['AGLs'])
            vcopy(AGL[:, 0, 8:16], APROD[:], [('APROD', c) for c in range(8)], ['AGLs'])
            dma(agl_in, AGL[:, 0, :], ['AGLs'], ['agl_in'])
            S.add('gpsimd', lambda e: e.collective_compute("AllGather", ALU.bypass, replica_groups=RG,
                                                           ins=[agl_in.opt()], outs=[agl_out.opt()]),
                  ['agl_in'], ['agl_out'])
            pump_casts(31 if layer == 0 else 10000)
            stg = STG_S[:].rearrange("p a b c -> p (a b c)")
            dma(stg[:, 0:96], st_lc[:, j, :, :, :].rearrange("p a b c -> p (a b c)"), [], ['STG_S'])
            dma(stg[:, 96:128], st_lh[:, j, :, :].rearrange("p a b -> p (a b)"), [], ['STG_S2'])
            vcopy(CARRY_L[:, :, 1:5, :], stg[:, 0:96].rearrange("p (a b c) -> p a b c", a=8, b=4), ['STG_S'], ck_all)
            vcopy(HST[:, :, 1:5], stg[:, 96:128].rearrange("p (a b) -> p a b", a=8), ['STG_S2'], hk_all)
            lru_tile(layer, tiles[-1], True, src, dst, False)
            dma(AGL[:], agl_out.rearrange("(r p) f -> p r f", p=128), ['agl_out', 'AGLs'], ['AGL'])
            V(lambda e: e.memset(HST[:, :, 0:1], 0.0), ['AGLs'], hk_all)
            h0 = HST[:, :, 0]
            for r in range(4):
                ts(COEF[:, 0:8], AGL[:, r, 8:16], MV[:, r:r + 1], OMV[:, r:r + 1], ALU.mult, ALU.add,
                   ['AGL', 'MV', 'OMV'], ['COEF'])
                tt(h0, h0, COEF[:, 0:8], ALU.mult, hk_all + ['COEF'], hk_all)
                stt(h0, AGL[:, r, 0:8], MV[:, r:r + 1], h0, ALU.mult, ALU.add, ['AGL', 'MV'] + hk_all, hk_all)
            vcopy(CARRY_L[:, :, 0, :], TAIL_L[:], [('TAIL_L', c) for c in range(8)], ck_all)
            for tile in ptiles:
                lru_tile(layer, tile, True, src, dst, False)
            dma(o_lh[:, j, :, :], HST[:], hk_all, [('o_lh', j)])
            dma(o_lc[:, j, :, :, :], CARRY_L[:], ck_all, [('o_lc', j)])

        def ssd_tile(layer, tile, full, src, dst, first):
            j = jidx[layer]
            col0, N, nseq, L, seq0 = tile
            sample = nseq > 1
            nch = 1 if sample else N // 64
            nsub = 4 if sample else 1
            Um, Tm, CMm = (cm('U16'), cm('T16'), cm('CM16')) if sample else (cm('U64'), cm('T64'), cm('CM64'))
            xt, xkey = load_x(src, tile)
            prenorm(xt[:, :, :N], xkey, N, 'nmp', layer)
            hk = [('HN', k) for k in range(8)]
            si_ = pstate['slot']
            pstate['slot'] = (si_ + 1) % len(SLOTS)
            wdk = ('slot', si_)
            assert S.recording or ('wb', 'wdt', j) in S.last_w
            dma(SLOTS[si_][:, 0:256], wdt_bf[j], [('wb', 'wdt', j)], [wdk])

            def wdt(pi, k, m0, m1, _b=SLOTS[si_]):
                return _b[:, k * 32 + m0:k * 32 + m1]
            for cc in range(nch):
                bank, bk = pb()
                for k in range(8):
                    mm(bank[:64, :32], HN[:, k, cc * 64:cc * 64 + 64], wdt(0, k, 0, 32), k == 0, k == 7, [wdk, hk[k]], [bk])
                tt(DT[:, cc, :], bank[:64, :32], PR[0:64, PRO['dtb'] + j * 32:PRO['dtb'] + j * 32 + 32], ALU.add,
                   [bk, 'PR'], ['DT'])
            dtv = DT[:, 0:nch, :]
            act(dtv, dtv, AF.Exp, ['DT'], ['DT'])
            act(dtv, dtv, AF.Ln, ['DT', 'KC'], ['DT'], bias=KC[0:64, 1:2])
            abc = ABC[0:64, j * 32:j * 32 + 32].unsqueeze(1).broadcast_to([64, nch, 32])
            tt(DA[:, 0:nch, :], dtv, abc, ALU.mult, ['DT', 'ABC'], ['DA'])
            if full:
                ev = PR[0:64, PRO['even']:PRO['even'] + 32].unsqueeze(1).broadcast_to([64, nch, 32])
                od = PR[0:64, PRO['odd']:PRO['odd'] + 32].unsqueeze(1).broadcast_to([64, nch, 32])
                tt(DTE[:, 0:nch, :], dtv, ev, ALU.mult, ['DT', 'PR'], ['DTE'])
                tt(DTO[:, 0:nch, :], dtv, od, ALU.mult, ['DT', 'PR'], ['DTO'])
            for cc in range(nch):
                bank, bk = pb()
                mm(bank[:64, :32], Um, DA[:, cc, :], True, True, ['CN', 'DA'], [bk])
                act(DEC[:, cc, :], bank[:64, :32], AF.Exp, [bk], ['DEC'])
                for q in range(nsub):
                    bank, bk = pb()
                    mi = (1 + q) if sample else 0
                    mrep = CN[0:64, CO['MREP'] + mi * 128:CO['MREP'] + mi * 128 + 128]
                    mm(bank[:, :32], mrep, DA[:, cc, :], True, True, ['CN', 'DA'], [bk])
                    act(BD[:, cc * nsub + q, :], bank[:, :32], AF.Exp, [bk], ['BD'])
                    if not full:
                        tt(BDTOT[:], BDTOT[:], BD[:, cc * nsub + q, :], ALU.mult, ['BD', 'BDTOT'], ['BDTOT'])
            tt(DTDEC[:, 0:nch, :], dtv, DEC[:, 0:nch, :], ALU.mult, ['DT', 'DEC'], ['DTDEC'])

            for g in range(4):
                hs = slice(8 * g, 8 * g + 8)
                if g == 2:
                    prefetch_x()
                wz = wzk = None
                if full:
                    wz, wzk = load_unit([('ssd_w_in', j, 0, DM, 512 * g, 512), ('ssd_w_in', j, 0, DM, 4608 + 128 * g, 128)])
                    for m in range(4):
                        bank, bk = pb()
                        for k in range(8):
                            mm(bank[:, :N], wz(0, k, m * 128, m * 128 + 128), HN[:, k, :N], k == 0, k == 7, [wzk, hk[k]], [bk])
                        act(ZG[:, m, :N], bank[:, :N], AF.Silu, [bk], [('SQ', 4 + m)])
                elif first:
                    wz, wzk = load_unit([('ssd_w_in', j, 0, DM, 4608 + 128 * g, 128)])
                w, wk = load_unit([('ssd_w_in', j, 0, DM, 2048 + 512 * g, 512), ('ssd_w_in', j, 0, DM, 4096 + 128 * g, 128)])
                chunks = [(w, wk, 0, m, 4 * g + m) for m in range(4)] + [(w, wk, 1, 0, 16 + g)]
                if full:
                    chunks.append((wz, wzk, 1, 0, 20 + g))
                elif first:
                    chunks.append((wz, wzk, 0, 0, 20 + g))
                pend = None
                for ci, (wa, wak, pi, m, cx) in enumerate(chunks):
                    if first:
                        bank, bk = pb()
                        for k in range(8):
                            mm(bank[:, :3], wa(pi, k, m * 128, m * 128 + 128), HNH[:, k, :3], k == 0, k == 7,
                               [wak, ('HNH', k)], [bk])
                        acopy(TAIL_S[:, cx, :], bank[:, :3], [bk], [('TAIL_S', cx)])
                        vcopy(CARRY_S[:, cx, 0, :], TAIL_S[:, cx, :], [('TAIL_S', cx)], [('CARRY_S', cx)])
                    if ci == 5 and not full:
                        continue
                    bank, bk = pb()
                    for k in range(8):
                        mm(bank[:, :N], wa(pi, k, m * 128, m * 128 + 128), HN[:, k, :N], k == 0, k == 7, [wak, hk[k]], [bk])
                    r = ci % 2
                    conv_chunk(bank, bk, CBS[r][:, :], ('CBS', 0, r), XCS[r][:, :N], ('XCS', r),
                               CARRY_S[:, cx, seq0:seq0 + nseq, :], ('CARRY_S', cx), N, nseq, L,
                               'scw', 'scb', lambda k, cx=cx: (j * 4 + k) * 24 + cx, j * 24 + cx)
                    if pend is not None:
                        act(XBC[:, pend[0], :N], XCS[pend[1]][:, :N], AF.Silu, [('XCS', pend[1])], [('XBC', pend[0])])
                    pend = (ci, r)
                if pend is not None:
                    act(XBC[:, pend[0], :N], XCS[pend[1]][:, :N], AF.Silu, [('XCS', pend[1])], [('XBC', pend[0])])
                if sample:
                    skeys = [('SP', q) for q in range(4)]
                    for q in range(4):
                        dma(SP[:, q, :], st_ss[j, q, :, 512 * g:512 * g + 512], [], [skeys[q]])
                    Sst = [SP[:, q, :] for q in range(4)]
                else:
                    skeys = [('SP', g)]
                    Sst = [SP[:, g, :]]
                Lq = 64 // nsub

                def h8(t):
                    return t.rearrange("p (h q) -> p h q", h=8)

                def b3(t, cc):
                    return t[:, cc, hs].unsqueeze(2).broadcast_to([64, 8, 64])

                def stage_a1a(cc):
                    T_ = TS[cc % 2]
                    k_ = lambda n: (n, cc % 2)
                    cs = slice(cc * 64, cc * 64 + 64)
                    hd = {}
                    if full:
                        tt(h8(T_['RHSD']), b3(DA, cc), Tm.unsqueeze(1).broadcast_to([64, 8, 64]), ALU.mult,
                           ['DA', 'CN'], [k_('RHSD')])
                        vcopy(h8(T_['DAX']), b3(DA, cc), ['DA'], [k_('DAX')])
                    tb, tbk = ptb()
                    for m in range(4):
                        P(lambda e, tb=tb, m=m, cs=cs: e.transpose(tb[0:64, m * 128:m * 128 + 128], XBC[:, m, cs], IDB[:]),
                          [('XBC', m), 'IDB'], [tbk])
                    P(lambda e, tb=tb, cs=cs: e.transpose(tb[0:64, 512:640], XBC[:, 4, cs], IDB[:]),
                      [('XBC', 4), 'IDB'], [tbk])
                    hd['tb'] = (tb, tbk)
                    return hd

                def stage_a1b(cc, hd):
                    T_ = TS[cc % 2]
                    k_ = lambda n: (n, cc % 2)
                    tb, tbk = hd['tb']
                    acopy(T_['XTOK'], tb[0:64, 0:512], [tbk], [k_('XTOK')])
                    acopy(T_['BTOK'], tb[0:64, 512:640], [tbk], [k_('BTOK')])
                    x3 = h8(T_['XTOK'])
                    tt(h8(T_['XDTD']), x3, b3(DTDEC, cc), ALU.mult, [k_('XTOK'), 'DTDEC'], [k_('XDTD')])
                    if full:
                        tt(h8(T_['XDTE']), x3, b3(DTE, cc), ALU.mult, [k_('XTOK'), 'DTE'], [k_('XDTE')])
                        tt(h8(T_['XDTO']), x3, b3(DTO, cc), ALU.mult, [k_('XTOK'), 'DTO'], [k_('XDTO')])
                        cs = slice(cc * 64, cc * 64 + 64)
                        bank, bk = pb()
                        mm(bank[:64, :64], XBC[:, 4, cs], XBC[:, 5, cs], True, True, [('XBC', 4), ('XBC', 5)], [bk])
                        hd['cb'] = (bank, bk)
                        bank, bk = pb()
                        mm(bank[:64, :512], Um, T_['RHSD'], True, True, ['CN', k_('RHSD')], [bk])
                        hd['D'] = (bank, bk)
                        bank, bk = pb()
                        for fc in range(4):
                            mm(bank[:, fc * 64:fc * 64 + 64], T_['DAX'][:, fc * 128:fc * 128 + 128], Tm, True, True,
                               [k_('DAX'), 'CN'], [bk])
                        hd['acs'] = (bank, bk)

                def stage_a2(cc, hd):
                    T_ = TS[cc % 2]
                    k_ = lambda n: (n, cc % 2)
                    bank, bk = hd['cb']
                    tt(T_['CBM'], bank[:64, :64], CMm, ALU.mult, [bk, 'CN'], [k_('CBM')])
                    bank, bk = hd['D']
                    act(T_['LT'], bank[:64, :512], AF.Exp, [bk], [k_('LT')])
                    bank, bk = hd['acs']
                    act(T_['EACS'], bank[:, :256], AF.Exp, [bk], [k_('EACS')])
                    tt(h8(T_['MT']), h8(T_['LT']), T_['CBM'].unsqueeze(1).broadcast_to([64, 8, 64]), ALU.mult,
                       [k_('LT'), k_('CBM')], [k_('MT')])
                    bd_, bdk = pb()
                    for fc in range(4):
                        mm(bd_[:, fc * 64:fc * 64 + 64], T_['XDTE'][:, fc * 128:fc * 128 + 128],
                           T_['MT'][:, (2 * fc) * 64:(2 * fc) * 64 + 64], True, False, [k_('XDTE'), k_('MT')], [bdk])
                        mm(bd_[:, fc * 64:fc * 64 + 64], T_['XDTO'][:, fc * 128:fc * 128 + 128],
                           T_['MT'][:, (2 * fc + 1) * 64:(2 * fc + 1) * 64 + 64], False, True, [k_('XDTO'), k_('MT')], [bdk])
                    hd['yd'] = (bd_, bdk)

                def stage_b(cc, hd):
                    T_ = TS[cc % 2]
                    k_ = lambda n: (n, cc % 2)
                    cs = slice(cc * 64, cc * 64 + 64)
                    if full:
                        bo, bok = pb()
                        for q in range(nsub):
                            for fc in range(4):
                                mm(bo[:, fc * 64 + q * Lq:fc * 64 + (q + 1) * Lq], SBF[:, q, fc * 128:fc * 128 + 128],
                                   XBC[:, 5, cc * 64 + q * Lq:cc * 64 + (q + 1) * Lq], True, True,
                                   [('SBF', q), ('XBC', 5)], [bok])
                    sbanks = []
                    for q in range(nsub):
                        if sample:
                            ts(BTM[:, q, :], T_['BTOK'], CN[0:64, CO['SUBM'] + q:CO['SUBM'] + q + 1], None, ALU.mult,
                               ALU.bypass, [k_('BTOK'), 'CN'], [('BTM', q)])
                            lhs, lk = BTM[:, q, :], ('BTM', q)
                        else:
                            lhs, lk = T_['BTOK'], k_('BTOK')
                        bank, bk = pb()
                        mm(bank[:, :512], lhs, T_['XDTD'], True, True, [lk, k_('XDTD')], [bk])
                        sbanks.append((bank, bk))
                    if full:
                        bd_, bdk = hd['yd']
                        tt(T_['YTMP'], bo[:, :256], T_['EACS'], ALU.mult, [bok, k_('EACS')], [k_('YTMP')])
                        tt(YY[:, :, cs], T_['YTMP'].rearrange("p (f q) -> p f q", f=4),
                           bd_[:, :256].rearrange("p (f q) -> p f q", f=4), ALU.add, [k_('YTMP'), bdk], ['YY'])
                    for q in range(nsub):
                        bank, bk = sbanks[q]
                        s3 = Sst[q].rearrange("p (h q) -> p h q", h=8)
                        tt(s3, s3, BD[:, cc * nsub + q, hs].unsqueeze(2).broadcast_to([128, 8, 64]), ALU.mult,
                           [skeys[q], 'BD', ('SBF', q)], [skeys[q]])
                        tt(Sst[q], Sst[q], bank[:, :512], ALU.add, [skeys[q], bk], [skeys[q]])
                    if full and cc + 1 < nch:
                        for q in range(nsub):
                            acopy(SBF[:, q, :], Sst[q], [skeys[q]], [('SBF', q)])

                if full:
                    for q in range(nsub):
                        acopy(SBF[:, q, :], Sst[q], [skeys[q]], [('SBF', q)])
                hd_cur = stage_a1a(0)
                stage_a1b(0, hd_cur)
                for cc in range(nch):
                    hd_next = stage_a1a(cc + 1) if cc + 1 < nch else None
                    if full:
                        stage_a2(cc, hd_cur)
                    if hd_next is not None:
                        stage_a1b(cc + 1, hd_next)
                    stage_b(cc, hd_cur)
                    hd_cur = hd_next
                if sample:
                    for q in range(4):
                        dma(o_ss[j, 1 + q, :, 512 * g:512 * g + 512], SP[:, q, :], [skeys[q]], [('o_ss', j, 1 + q, g)])
                if not full:
                    continue
                for fc in range(4):
                    stt(YY[:, fc, :N], XBC[:, fc, :N], pv('sdx', j * 16 + 4 * g + fc), YY[:, fc, :N], ALU.mult, ALU.add,
                        [('XBC', fc), 'YY', 'PV'], ['YY'])
                tt(YY[:, :, :N], YY[:, :, :N], ZG[:, :, :N], ALU.mult, ['YY'] + [('SQ', 4 + m) for m in range(4)], ['YY'])
                A(lambda e, N=N: e.activation(out=SQ[:, 0:4, :N], in_=YY[:, :, :N], func=AF.Square), ['YY'],
                  [('SQ', c) for c in range(4)])
                rstd_from_sq(4, N, 512, [('SQ', c) for c in range(4)])
                for fc in range(4):
                    stt(ACTB[:, 4 * g + fc, :N], YY[:, fc, :N], pv('sgn', j * 16 + 4 * g + fc), RSTD[:, :N], ALU.mult,
                        ALU.mult, ['YY', 'RSTD', 'PV'], [('ACTB', 4 * g + fc)])
            if not full:
                end_x()
                return
            for u in range(4):
                w, wk = load_unit([('ssd_w_out', j, 0, 2048, u * 256, 256)])
                for m in range(2):
                    bank, bk = pb()
                    for k in range(16):
                        mm(bank[:, :N], w(0, k, m * 128, m * 128 + 128), ACTB[:, k, :N], k == 0, k == 15,
                           [wk, ('ACTB', k)], [bk])
                    evac_post(bank, bk, 2 * u + m, N)
            postnorm_residual(xt, xkey, N, 'nmq', layer)
            store_x(dst, tile, xt, xkey)
            end_x()

        def ssd_layer(layer, src, dst):
            j = jidx[layer]
            spk = [('SP', g) for g in range(4)]
            ck_all = [('CARRY_S', c) for c in range(24)]
            V(lambda e: e.memset(SP[:], 0.0), [], spk)
            V(lambda e: e.memset(BDTOT[:], 1.0), [], ['BDTOT'])
            for ti, tile in enumerate(ptiles):
                ssd_tile(layer, tile, False, src, dst, ti == 0)
            dma(ags_in, SP[:].rearrange("p g f -> p (g f)"), spk, ['ags_in'])
            dma(agd_in, BDTOT[:], ['BDTOT'], ['agd_in'])
            S.add('gpsimd', lambda e: e.collective_compute("AllGather", ALU.bypass, replica_groups=RG,
                                                           ins=[ags_in.opt()], outs=[ags_out.opt()]),
                  ['ags_in'], ['ags_out'])
            S.add('gpsimd', lambda e: e.collective_compute("AllGather", ALU.bypass, replica_groups=RG,
                                                           ins=[agd_in.opt()], outs=[agd_out.opt()]),
                  ['agd_in', 'ags_out'], ['agd_out'])
            pump_casts(31 if layer == 0 else 10000)
            dma(STG_S[:], st_sc[:, j, :, :, :], [], ['STG_S'])
            vcopy(CARRY_S[:, :, 1:5, :], STG_S[:], ['STG_S'], ck_all)
            ssd_tile(layer, tiles[-1], True, src, dst, False)
            V(lambda e: e.memset(SP[:], 0.0), ['ags_in'], spk)
            for r in range(4):
                dma(AGT[:, 0:2048], ags_out[r * 128:(r + 1) * 128, :], ['ags_out'], ['AGT'])
                dma(AGT[:, 2048:2080], agd_out[r * 128:(r + 1) * 128, :], ['agd_out'], ['AGTd'])
                ts(COEF[:], AGT[:, 2048:2080], MV[:, r:r + 1], OMV[:, r:r + 1], ALU.mult, ALU.add,
                   ['AGTd', 'MV', 'OMV'], ['COEF'])
                for g in range(4):
                    s3 = SP[:, g, :].rearrange("p (h q) -> p h q", h=8)
                    tt(s3, s3, COEF[:, 8 * g:8 * g + 8].unsqueeze(2).broadcast_to([128, 8, 64]), ALU.mult,
                       [spk[g], 'COEF'], [spk[g]])
                    stt(SP[:, g, :], AGT[:, 512 * g:512 * g + 512], MV[:, r:r + 1], SP[:, g, :], ALU.mult, ALU.add,
                        ['AGT', 'MV', spk[g]], [spk[g]])
            vcopy(CARRY_S[:, :, 0, :], TAIL_S[:], [('TAIL_S', c) for c in range(24)], ck_all)
            for tile in ptiles:
                ssd_tile(layer, tile, True, src, dst, False)
            dma(o_ss[j, 0, :, :], SP[:].rearrange("p g f -> p (g f)"), spk, [('o_ss', j, 0)])
            dma(o_sc[:, j, :, :, :], CARRY_S[:], ck_all, [('o_sc', j)])

        def emit_all():
            halo_exchange(xin, 0)
            pump_casts(16, 8)
            for layer in range(depth):
                src = xin if layer == 0 else xsc
                if kinds[layer] == 'lru':
                    lru_layer(layer, src, xsc)
                else:
                    ssd_layer(layer, src, xsc)
                fdst = yout if layer == depth - 1 else xsc
                for ti, tile in enumerate(ptiles):
                    hn_ = layer + 1 if (ti == len(ptiles) - 1 and layer + 1 < depth) else None
                    ffn_tile(layer, tile, xsc, fdst, halo_next=hn_)
                ffn_tile(layer, tiles[-1], xsc, fdst)

        S.recording = True
        emit_all()
        S.recording = False
        for k_ in pstate:
            pstate[k_] = 0
        emit_all()
        assert xst['i'] == len(XSEQ)
        pump_casts(10000)
        S.emit(nc, st)
    return nc, S


def _fm(v):
    v = np.asarray(v, np.float32)
    nch = v.shape[-1] // 128
    v = v.reshape(v.shape[:-1] + (nch, 128))
    return np.ascontiguousarray(np.moveaxis(v, -1, 0))


def _consts():
    c = np.zeros((128, NCO), np.float32)
    c[:, 0:128] = np.eye(128, dtype=np.float32)
    k = np.arange(64)[:, None]
    i = np.arange(64)[None, :]
    c[0:64, CO['U64']:CO['U64'] + 64] = (k > i)
    c[0:64, CO['T64']:CO['T64'] + 64] = (k <= i)
    c[0:64, CO['CM64']:CO['CM64'] + 64] = (k <= i)
    same = (k // 16) == (i // 16)
    c[0:64, CO['U16']:CO['U16'] + 64] = (k > i) & same
    c[0:64, CO['T16']:CO['T16'] + 64] = (k <= i) & same
    c[0:64, CO['CM16']:CO['CM16'] + 64] = (k <= i) & same
    c[0:64, CO['MREP']:CO['MREP'] + 128] = 1.0
    for q in range(4):
        c[16 * q:16 * q + 16, CO['MREP'] + (1 + q) * 128:CO['MREP'] + (2 + q) * 128] = 1.0
        c[16 * q:16 * q + 16, CO['SUBM'] + q] = 1.0
    return c


_CACHE = {}
RUNNER = None


def kernel(**inp):
    return _run(inp, DEPTH)


def _run(inp, depth, kinds=None):
    inp = {k: np.asarray(v) for k, v in inp.items()}
    ck = (depth, tuple(kinds) if kinds else None)
    if ck not in _CACHE:
        _CACHE[ck] = build_program(depth, kinds)[0]
    nc = _CACHE[ck]
    f32 = np.float32
    xp = inp['x_prompt'].astype(f32, copy=False)
    xs = inp['x_sample'].astype(f32, copy=False)
    pvec = np.zeros((128, NPV), f32)

    def put(name, arr):
        a = arr.reshape(128, -1)
        pvec[:, PVO[name]:PVO[name] + a.shape[1]] = a
    put('nmp', _fm(inp['norm_mix_pre']))
    put('nmq', _fm(inp['norm_mix_post']))
    put('nfp', _fm(inp['norm_ffn_pre']))
    put('nfq', _fm(inp['norm_ffn_post']))
    put('lcw', _fm(inp['lru_conv_w']))
    put('lcb', _fm(inp['lru_conv_b']))
    put('lbr', _fm(inp['lru_b_r'].reshape(NA, 1024)))
    put('lbi', _fm(inp['lru_b_i'].reshape(NA, 1024)))
    put('llam', _fm(inp['lru_lambda']))
    put('scw', _fm(inp['ssd_conv_w']))
    put('scb', _fm(inp['ssd_conv_b']))
    put('sgn', _fm(inp['ssd_norm']))
    put('sdx', _fm(np.repeat(inp['ssd_d'], 64, axis=-1)))
    prow = np.zeros((128, NPR), f32)
    prow[:, PRO['dtb']:PRO['dtb'] + NB * 32] = inp['ssd_dt_bias'].reshape(1, -1)
    prow[:, PRO['alog']:PRO['alog'] + NB * 32] = inp['ssd_a_log'].reshape(1, -1)
    prow[:, PRO['even']:PRO['even'] + 32] = (np.arange(32) % 2 == 0).astype(f32)[None]
    prow[:, PRO['odd']:PRO['odd'] + 32] = (np.arange(32) % 2 == 1).astype(f32)[None]
    cons = _consts()
    shared = {
        'pvec': pvec, 'prow': prow, 'cons': cons,
        'lru_w_in': inp['lru_w_in'], 'lru_w_r': inp['lru_w_r'].reshape(NA, 1024, 256),
        'lru_w_i': inp['lru_w_i'].reshape(NA, 1024, 256), 'lru_w_out': inp['lru_w_out'],
        'ssd_w_in': inp['ssd_w_in'], 'ssd_w_out': inp['ssd_w_out'],
        'ffn_w_gate': inp['ffn_w_gate'], 'ffn_w_up': inp['ffn_w_up'], 'ffn_w_down': inp['ffn_w_down'],
    }
    in_maps = []
    for c in range(8):
        b, s = c // 4, c % 4
        xt = np.concatenate([xp[b, s * NPT:(s + 1) * NPT, :], xs[4 * c:4 * c + 4].reshape(64, DM)], axis=0)
        xin = np.ascontiguousarray(xt.T.reshape(8, 128, NTOK).transpose(1, 0, 2))
        mvec = np.zeros((128, 8), f32)
        for r in range(4):
            mvec[:, r] = 1.0 if r < s else 0.0
            mvec[:, 4 + r] = 1.0 if r == s - 1 else 0.0
        sl = slice(4 * c, 4 * c + 4)
        st_lc = np.ascontiguousarray(_fm(inp['state_lru_conv'][:, sl]).transpose(0, 1, 4, 2, 3))
        st_lh = np.ascontiguousarray(_fm(inp['state_lru_h'][:, sl]).transpose(0, 1, 3, 2))
        st_sc = np.ascontiguousarray(_fm(inp['state_ssd_conv'][:, sl]).transpose(0, 1, 4, 2, 3))
        st_ss = np.ascontiguousarray(inp['state_ssd'][:, sl].reshape(NB, 4, 2048, 128).transpose(0, 1, 3, 2))
        m = dict(shared)
        m.update({'xin': xin, 'mvec': mvec, 'st_lc': st_lc, 'st_lh': st_lh, 'st_sc': st_sc, 'st_ss': st_ss})
        in_maps.append(m)
    if RUNNER is not None:
        R = RUNNER(nc, in_maps)
    else:
        R = run_bass_kernel_spmd(nc, in_maps, core_ids=list(range(8))).results
    y_prompt = np.zeros((2, 8192, DM), f32)
    y_sample = np.zeros((32, 16, DM), f32)
    p_lc = np.zeros((NA, 2, 3, 1024), f32)
    p_lh = np.zeros((NA, 2, 1024), f32)
    p_sc = np.zeros((NB, 2, 3, 3072), f32)
    p_ss = np.zeros((NB, 2, 32, 64, 128), f32)
    s_lc = np.zeros((NA, 32, 3, 1024), f32)
    s_lh = np.zeros((NA, 32, 1024), f32)
    s_sc = np.zeros((NB, 32, 3, 3072), f32)
    s_ss = np.zeros((NB, 32, 32, 64, 128), f32)

    def unfm(a):
        a = np.moveaxis(a, 0, -1)
        return a.reshape(a.shape[:-2] + (a.shape[-2] * 128,))
    for c in range(8):
        b, s = c // 4, c % 4
        y = np.asarray(R[c]['y'])
        yt = y.transpose(2, 1, 0).reshape(NTOK, DM)
        y_prompt[b, s * NPT:(s + 1) * NPT] = yt[:NPT]
        y_sample[4 * c:4 * c + 4] = yt[NPT:].reshape(4, 16, DM)
        lc = unfm(np.asarray(R[c]['o_lc']).transpose(0, 1, 3, 4, 2))
        lh = unfm(np.asarray(R[c]['o_lh']).transpose(0, 1, 3, 2))
        sc = unfm(np.asarray(R[c]['o_sc']).transpose(0, 1, 3, 4, 2))
        ss = np.asarray(R[c]['o_ss']).transpose(0, 1, 3, 2).reshape(NB, 5, 32, 64, 128)
        s_lc[:, 4 * c:4 * c + 4] = lc[:, 1:5]
        s_lh[:, 4 * c:4 * c + 4] = lh[:, 1:5]
        s_sc[:, 4 * c:4 * c + 4] = sc[:, 1:5]
        s_ss[:, 4 * c:4 * c + 4] = ss[:, 1:5]
        if s == 3:
            p_lc[:, b] = lc[:, 0]
            p_lh[:, b] = lh[:, 0]
            p_sc[:, b] = sc[:, 0]
            p_ss[:, b] = ss[:, 0]
    return (y_prompt, y_sample, p_lc, p_lh, p_sc, p_ss, s_lc, s_lh, s_sc, s_ss)
```
